# Optimizing a Trainium2 kernel written in Bass

```python
import jax, jax.numpy as jnp
from jax import lax
import numpy as np

D_MODEL = 1024
BATCH = 1
SEQ = 16384
DEPTH = 1

PLE_DIM = 256
N_HEADS = 8
QK_NOPE = 64
QK_ROPE = 32
V_HEAD = 64
Q_LORA = 384
KV_LORA = 256
POOL_WINDOWS = (2, 4, 8, 16)
POOL_GROUP = 128
POOL_WIDTH = POOL_GROUP * len(POOL_WINDOWS)
N_BRANCH = 2
D_FF = 4 * D_MODEL
ROPE_THETA = 10000.0
EPS = 1e-6
Q_BLOCK = 128
IN_SPLITS = (Q_LORA, KV_LORA, QK_ROPE, POOL_WIDTH, D_MODEL, D_MODEL)
IN_WIDTH = sum(IN_SPLITS)

kernel_name = "hybrid_mla_multiscale_pool_gated_block"


def rmsnorm(x, g):
    xf = x.astype(jnp.float32)
    y = xf * lax.rsqrt(jnp.mean(xf * xf, axis=-1, keepdims=True) + EPS)
    return (y * g.astype(jnp.float32)).astype(x.dtype)


def rope_cos_sin(positions, dim, dtype):
    inv_freq = ROPE_THETA ** (-jnp.arange(0, dim, 2, dtype=jnp.float32) / dim)
    ang = positions.astype(jnp.float32)[..., None] * inv_freq
    return jnp.cos(ang).astype(dtype), jnp.sin(ang).astype(dtype)


def apply_rope(x, cos, sin):
    half = x.shape[-1] // 2
    x1, x2 = x[..., :half], x[..., half:]
    return jnp.concatenate([x1 * cos - x2 * sin, x2 * cos + x1 * sin], axis=-1)


def mla_attention(q_nope, q_rope, k_nope, k_rope, v):
    B, S, H, _ = q_nope.shape
    nb = S // Q_BLOCK
    scale = (QK_NOPE + QK_ROPE) ** -0.5
    kpos = jnp.arange(S)

    def block(args):
        qn, qr, i = args
        s = jnp.einsum('bqhd,bkhd->bhqk', qn, k_nope, preferred_element_type=jnp.float32)
        s = s + jnp.einsum('bqhr,bkr->bhqk', qr, k_rope, preferred_element_type=jnp.float32)
        qpos = i * Q_BLOCK + jnp.arange(Q_BLOCK)
        mask = kpos[None, :] <= qpos[:, None]
        s = jnp.where(mask, s * scale, -jnp.inf)
        pr = jax.nn.softmax(s, axis=-1).astype(v.dtype)
        return jnp.einsum('bhqk,bkhd->bqhd', pr, v)

    qn_b = q_nope.reshape(B, nb, Q_BLOCK, H, QK_NOPE).transpose(1, 0, 2, 3, 4)
    qr_b = q_rope.reshape(B, nb, Q_BLOCK, H, QK_ROPE).transpose(1, 0, 2, 3, 4)
    out = lax.map(block, (qn_b, qr_b, jnp.arange(nb)))
    return out.transpose(1, 0, 2, 3, 4).reshape(B, S, H * V_HEAD)


def multiscale_pool(u, w_pool, pool_scale):
    B, S, _ = u.shape
    uf = u.reshape(B, S, len(POOL_WINDOWS), POOL_GROUP).astype(jnp.float32)
    cs = jnp.cumsum(uf, axis=1)
    t = jnp.arange(S)
    outs = []
    for g, w in enumerate(POOL_WINDOWS):
        c = cs[:, :, g]
        prev = jnp.pad(c, ((0, 0), (w, 0), (0, 0)))[:, :S]
        cnt = jnp.minimum(t + 1, w).astype(jnp.float32)[None, :, None]
        outs.append((c - prev) / cnt - uf[:, :, g])
    d = jnp.stack(outs, axis=2).astype(u.dtype)
    y = jnp.einsum('bsgc,gcd->bsgd', d, w_pool).reshape(B, S, POOL_WIDTH)
    return y * pool_scale


def setup_inputs(seed: int = 0) -> dict:
    key = jax.random.key(seed)
    ks = jax.random.split(key, 24)

    def w(k, shape, fan_in):
        return jax.random.normal(k, shape, jnp.float32) * (fan_in ** -0.5)

    def gain(k, shape):
        return 1.0 + 0.05 * jax.random.normal(k, shape, jnp.float32)

    L = DEPTH
    x = jax.random.normal(ks[0], (BATCH, SEQ, D_MODEL), jnp.float32)
    p = jax.random.normal(ks[1], (DEPTH, BATCH, SEQ, PLE_DIM), jnp.float32)
    offset = jax.random.randint(ks[2], (BATCH, 1), 0, 4096, dtype=jnp.int32)
    positions = offset + jnp.arange(SEQ, dtype=jnp.int32)[None, :]
    return {
        "x": x,
        "p": p,
        "positions": positions,
        "g_pre_mix": gain(ks[3], (L, D_MODEL)),
        "w_in": w(ks[4], (L, D_MODEL, IN_WIDTH), D_MODEL),
        "b_gate": 0.01 * jax.random.normal(ks[5], (L, N_BRANCH * D_MODEL), jnp.float32),
        "g_q": gain(ks[6], (L, Q_LORA)),
        "w_uq": w(ks[7], (L, Q_LORA, N_HEADS * (QK_NOPE + QK_ROPE)), Q_LORA),
        "g_kv": gain(ks[8], (L, KV_LORA)),
        "w_ukv": w(ks[9], (L, KV_LORA, N_HEADS * (QK_NOPE + V_HEAD)), KV_LORA),
        "w_pool": w(ks[10], (L, len(POOL_WINDOWS), POOL_GROUP, POOL_GROUP), POOL_GROUP),
        "pool_scale": gain(ks[11], (L, POOL_WIDTH)),
        "w_branch_attn": w(ks[12], (L, N_HEADS * V_HEAD, D_MODEL), N_HEADS * V_HEAD),
        "w_branch_pool": w(ks[13], (L, POOL_WIDTH, D_MODEL), POOL_WIDTH),
        "w_out": w(ks[14], (L, D_MODEL, D_MODEL), D_MODEL),
        "g_post_mix": gain(ks[15], (L, D_MODEL)),
        "g_pre_mlp": gain(ks[16], (L, D_MODEL)),
        "w_ff1": w(ks[17], (L, D_MODEL, D_FF), D_MODEL),
        "w_ff2": w(ks[18], (L, D_FF, D_MODEL), D_FF),
        "g_post_mlp": gain(ks[19], (L, D_MODEL)),
        "w_ple_proj": w(ks[20], (L, PLE_DIM, D_MODEL), PLE_DIM),
        "w_ple_gate": w(ks[21], (L, D_MODEL, D_MODEL), D_MODEL),
        "g_ple": gain(ks[22], (L, D_MODEL)),
    }


def reference(x, p, positions, g_pre_mix, w_in, b_gate, g_q, w_uq, g_kv, w_ukv,
              w_pool, pool_scale, w_branch_attn, w_branch_pool, w_out, g_post_mix,
              g_pre_mlp, w_ff1, w_ff2, g_post_mlp, w_ple_proj, w_ple_gate, g_ple):
    B, S, _ = x.shape
    cos, sin = rope_cos_sin(positions, QK_ROPE, x.dtype)
    split_idx = list(np.cumsum(IN_SPLITS)[:-1])
    h = x
    for i in range(DEPTH):
        a = rmsnorm(h, g_pre_mix[i])
        proj = a @ w_in[i]
        q_down, kv_down, k_rope, pool_in, gates = (
            *jnp.split(proj, split_idx, axis=-1)[:4],
            proj[..., sum(IN_SPLITS[:4]):])
        gates = jax.nn.sigmoid(gates + b_gate[i])
        gate_attn, gate_pool = gates[..., :D_MODEL], gates[..., D_MODEL:]

        q = (rmsnorm(q_down, g_q[i]) @ w_uq[i]).reshape(B, S, N_HEADS, QK_NOPE + QK_ROPE)
        q_nope = q[..., :QK_NOPE]
        q_rope = apply_rope(q[..., QK_NOPE:], cos[:, :, None, :], sin[:, :, None, :])
        kv = (rmsnorm(kv_down, g_kv[i]) @ w_ukv[i]).reshape(B, S, N_HEADS, QK_NOPE + V_HEAD)
        k_nope, v = kv[..., :QK_NOPE], kv[..., QK_NOPE:]
        k_rope = apply_rope(k_rope, cos, sin)
        attn = mla_attention(q_nope, q_rope, k_nope, k_rope, v)

        pooled = multiscale_pool(pool_in, w_pool[i], pool_scale[i])

        merged = gate_attn * (attn @ w_branch_attn[i]) + gate_pool * (pooled @ w_branch_pool[i])
        h = h + rmsnorm(merged @ w_out[i], g_post_mix[i])

        m = rmsnorm(h, g_pre_mlp[i])
        f = jnp.square(jax.nn.relu(m @ w_ff1[i])) @ w_ff2[i]
        h = h + rmsnorm(f, g_post_mlp[i])

        e = p[i] @ w_ple_proj[i]
        pg = jax.nn.sigmoid(h @ w_ple_gate[i])
        h = h + rmsnorm(pg * e, g_ple[i])
    return h
```

```python
import math
import contextlib
import numpy as np
import concourse.bass as bass
import concourse.mybir as mybir
from concourse.bass_utils import run_bass_kernel_spmd

F32 = mybir.dt.float32
BF16 = mybir.dt.bfloat16
I32 = mybir.dt.int32
ALU = mybir.AluOpType
AF = mybir.ActivationFunctionType

NCORES = 8
D = 1024
EPS = 1e-6
ENGS = ("pe", "act", "dve", "pool", "sp")


class Tok:
    __slots__ = ("w", "r")

    def __init__(self):
        self.w = []
        self.r = []


def A(t):
    return ("+", t)


class Op:
    __slots__ = ("eng", "fn", "deps", "is_dma", "signal", "clock", "needs_inc", "idx")

    def __init__(self, eng, fn, deps, is_dma):
        self.eng = eng
        self.fn = fn
        self.deps = deps
        self.is_dma = is_dma
        self.signal = None
        self.clock = None
        self.needs_inc = is_dma
        self.idx = 0


class Prog:
    def __init__(self, nc, n_dma_sems=48):
        self.nc = nc
        self.ops = []
        self.per_eng = {e: [] for e in ENGS}
        self.n_dma_sems = n_dma_sems
        self.dma_rr = 0
        self.dma_last = [None] * n_dma_sems
        self.dma_cnt = [0] * n_dma_sems

    def tok(self):
        return Tok()

    def _mk(self, eng, fn, reads, writes, is_dma, extra=()):
        deps = []
        seen = set()

        def add(o):
            if o is None or id(o) in seen:
                return
            if eng == "pe" and o.eng == "pe" and not o.is_dma:
                return
            seen.add(id(o))
            deps.append(o)

        for t in reads:
            for o in t.w:
                add(o)
        for t in writes:
            if isinstance(t, tuple):
                for o in t[1].r:
                    add(o)
            else:
                for o in t.r:
                    add(o)
                for o in t.w:
                    add(o)
        for o in extra:
            add(o)
        op = Op(eng, fn, deps, is_dma)
        for d in deps:
            d.needs_inc = True
        for t in reads:
            t.r.append(op)
        for t in writes:
            if isinstance(t, tuple):
                t[1].w.append(op)
            else:
                t.w = [op]
                t.r = []
        op.idx = len(self.ops)
        self.ops.append(op)
        self.per_eng[eng].append(op)
        return op

    def op(self, eng, fn, reads=(), writes=(), extra=()):
        return self._mk(eng, fn, reads, writes, False, extra)

    def dma(self, eng, out, in_, reads=(), writes=()):
        s = self.dma_rr % self.n_dma_sems
        self.dma_rr += 1
        prev = self.dma_last[s]
        self.dma_cnt[s] += 1
        val = 16 * self.dma_cnt[s]

        def fn(e, out=out, in_=in_):
            return e.dma_start(out=out, in_=in_)

        op = self._mk(eng, fn, reads, writes, True, extra=(prev,) if prev is not None else ())
        op.signal = (("dma", s), val)
        self.dma_last[s] = op
        return op

    def barrier(self):
        lasts = []
        for e in ENGS:
            for o in reversed(self.per_eng[e]):
                if not o.is_dma:
                    lasts.append(o)
                    break
        lasts += [o for o in self.dma_last if o is not None]
        for e in ENGS:
            self._mk(e, lambda eng: eng.nop(), (), (), False, extra=lasts)

    def emit(self):
        nc = self.nc
        cnt = {e: 0 for e in ENGS}
        for e in ENGS:
            for op in self.per_eng[e]:
                if op.is_dma:
                    continue
                if op.needs_inc:
                    cnt[e] += 1
                    op.signal = (("eng", e), cnt[e])
        known = {e: {} for e in ENGS}
        waits = {}
        for op in self.ops:
            kn = known[op.eng]
            best = {}
            for d in op.deps:
                k, v = d.signal
                if kn.get(k, 0) < v:
                    if best.get(k, 0) < v:
                        best[k] = v
                    for kk, vv in d.clock.items():
                        if kn.get(kk, 0) < vv:
                            kn[kk] = vv
            waits[op.idx] = list(best.items())
            if op.signal is not None:
                c = dict(kn)
                k, v = op.signal
                if c.get(k, 0) < v:
                    c[k] = v
                op.clock = c
        with contextlib.ExitStack() as st:
            sems = {}
            for e in ENGS:
                sems[("eng", e)] = st.enter_context(nc.semaphore(f"s_{e}"))
            for i in range(self.n_dma_sems):
                sems[("dma", i)] = st.enter_context(nc.semaphore(f"s_dma{i}"))
            block = st.enter_context(nc.Block())
            engmap = {"pe": block.tensor, "act": block.scalar, "dve": block.vector,
                      "pool": block.gpsimd, "sp": block.sync}

            def make(e):
                ops = self.per_eng[e]

                def body(eng):
                    for op in ops:
                        for k, v in waits[op.idx]:
                            eng.wait_ge(sems[k], v)
                        ins = op.fn(eng)
                        if op.signal is not None:
                            k, v = op.signal
                            ins.then_inc(sems[k], 16 if op.is_dma else 1)
                return body

            for e in ENGS:
                if self.per_eng[e]:
                    engmap[e](make(e))


class Rot:
    def __init__(self, items):
        self.items = items
        self.i = 0

    def next(self):
        it = self.items[self.i % len(self.items)]
        self.i += 1
        return it


V_GPM, V_GQ, V_GKV, V_BG, V_PSC, V_GPOM, V_GPRM, V_GPOL, V_GPLE, V_INVF, V_N = 0, 8, 11, 13, 29, 33, 41, 49, 57, 65, 66


def build(R, debug=False):
    NJ = R
    T = 128 * NJ
    S = 1024 * R
    NB = 8 * R
    NP = R // 4
    assert R % 4 == 0
    nc = bass.Bass("TRN2", target_bir_lowering=False)

    def din(name, shape, dt=F32):
        return nc.dram_tensor(name, list(shape), dt, kind="ExternalInput").ap()

    def dscr(name, shape, dt):
        return nc.dram_tensor(name, list(shape), dt, kind="ExternalOutput" if debug else "Internal").ap()

    x_all = din("x_all", [S, D])
    x_own = din("x_own", [T, D])
    x_halo = din("x_halo", [16 * NJ, D])
    p_own = din("p_own", [T, 256])
    pos_all = din("pos_all", [1, S], I32)
    pos_own = din("pos_own", [1, T], I32)
    maskb_d = din("maskb", [128, 8 * 128])
    vecs_d = din("vecs", [128, V_N])
    band0_d = din("band0", [128, 4 * 128])
    bandN_d = din("bandN", [128, 4 * 128])
    bandH_d = din("bandH", [16, 4 * 128])
    w_q = din("w_q", [D, 384])
    w_kv = din("w_kv", [D, 256])
    w_krA = din("w_krA", [D, 96])
    w_krB = din("w_krB", [D, 96])
    w_pin = din("w_pin", [D, 512])
    w_gate = din("w_gate", [D, 2048])
    w_uqA = din("w_uqA", [384, 8 * 96])
    w_uqB = din("w_uqB", [384, 8 * 96])
    w_uk = din("w_uk", [256, 512])
    w_uv = din("w_uv", [256, 512])
    w_pool = din("w_pool", [512, 128])
    w_ba = din("w_ba", [512, D])
    w_bp = din("w_bp", [512, D])
    w_out = din("w_out", [D, D])
    w_ff1 = din("w_ff1", [D, 4096])
    w_ff2 = din("w_ff2", [4096, D])
    w_pe = din("w_pe", [256, D])
    w_pg = din("w_pg", [D, D])
    out_own = nc.dram_tensor("out_own", [T, D], F32, kind="ExternalOutput").ap()

    kscr = dscr("kscr", [8, 96, S], BF16)
    vscr = dscr("vscr", [8, 128, NB * 65], BF16)
    hscr = dscr("hscr", [128, 8 * T], F32)
    tab_all = dscr("tab_all", [3, 16, S], F32)
    tab_own = dscr("tab_own", [3, 16, T], F32)
    if debug:
        qt_dbg = nc.dram_tensor("qt_dbg", [96, 8 * T], BF16, kind="ExternalOutput").ap()
        at_dbg = nc.dram_tensor("at_dbg", [64, 8 * T], BF16, kind="ExternalOutput").ap()

    P = Prog(nc)

    def mm(out, lhsT, rhs, start, stop, reads, writes):
        return P.op("pe", lambda e: e.matmul(out, lhsT=lhsT, rhs=rhs, start=start, stop=stop), reads, writes)

    def tr(out, in_, ident, reads, writes):
        return P.op("pe", lambda e: e.transpose(out=out, in_=in_, identity=ident), reads, writes)

    def act(out, in_, func, reads, writes, bias=None, scale=None):
        kw = {}
        if bias is not None:
            kw["bias"] = bias
        if scale is not None:
            kw["scale"] = scale
        return P.op("act", lambda e: e.activation(out=out, in_=in_, func=func, **kw), reads, writes)

    def ts(eng, out, in0, s1, s2, op0, op1, reads, writes):
        if op1 is None:
            return P.op(eng, lambda e: e.tensor_scalar(out=out, in0=in0, scalar1=s1, scalar2=None, op0=op0), reads, writes)
        return P.op(eng, lambda e: e.tensor_scalar(out=out, in0=in0, scalar1=s1, scalar2=s2, op0=op0, op1=op1), reads, writes)

    def tt(eng, out, in0, in1, op, reads, writes):
        return P.op(eng, lambda e: e.tensor_tensor(out=out, in0=in0, in1=in1, op=op), reads, writes)

    def stt(eng, out, in0, scalar, in1, op0, op1, reads, writes, accum_out=None):
        if accum_out is None:
            return P.op(eng, lambda e: e.scalar_tensor_tensor(out=out, in0=in0, scalar=scalar, in1=in1, op0=op0, op1=op1), reads, writes)
        return P.op(eng, lambda e: e.scalar_tensor_tensor(out=out, in0=in0, scalar=scalar, in1=in1, op0=op0, op1=op1,
                                                          accum_out=accum_out), reads, writes)

    def cp(eng, out, in_, reads, writes):
        return P.op(eng, lambda e: e.tensor_copy(out=out, in_=in_), reads, writes)

    def recip(out, in_, reads, writes):
        return P.op("dve", lambda e: e.reciprocal(out=out, in_=in_), reads, writes)

    def memset(eng, ap, val, writes):
        return P.op(eng, lambda e: e.memset(ap, val), (), writes)

    def rstd_from_sum(dst, src, n, reads, tokd):
        ts("dve", dst, src, 1.0 / n, EPS, ALU.mult, ALU.add, reads, [tokd])
        act(dst, dst, AF.Sqrt, [tokd], [tokd])
        recip(dst, dst, [tokd], [tokd])

    with contextlib.ExitStack() as st0:
        def sb(st, name, shape, dt):
            return st.enter_context(nc.sbuf_tensor(name, list(shape), dt))

        pp = [st0.enter_context(nc.psum_tensor(f"pp{i}", [128, 2, 512], F32)) for i in range(4)]
        bank_items = []
        for i in range(4):
            for j in range(2):
                bank_items.append((pp[i][:, j, :], P.tok()))
        banks = Rot(bank_items)

        identf = sb(st0, "identf", [128, 128], F32)
        identb = sb(st0, "identb", [128, 128], BF16)
        ones_bf = sb(st0, "ones_bf", [128, 128], BF16)
        sel = sb(st0, "sel", [65, 64], F32)
        vecs = sb(st0, "vecs_sb", [128, V_N], F32)
        t_c = P.tok()
        P.op("pool", lambda e: e.memset(identf[:], 1.0), (), [t_c])
        P.op("pool", lambda e: e.affine_select(out=identf[:], in_=identf[:], pattern=[[-1, 128]],
                                               compare_op=ALU.is_equal, fill=0.0, base=0, channel_multiplier=1),
             [t_c], [t_c])
        cp("dve", identb[:], identf[:], [t_c], [t_c])
        memset("dve", ones_bf[:], 1.0, [t_c])
        memset("dve", sel[0:64, :], 0.0, [t_c])
        memset("dve", sel[64:65, :], 1.0, [t_c])
        P.dma("sp", vecs[:], vecs_d, (), [t_c])

        def transpose_to(dst3, src_tok_ap, npart, ident, t_src, t_dst, evac_eng="act", f32=False):
            if f32:
                for half in range(2):
                    bk, tb = banks.next()
                    v = bk.rearrange("p (c t) -> p c t", c=4)
                    for c in range(4):
                        cc = half * 4 + c
                        tr(v[:, c, 0:npart], src_tok_ap[:, cc * 128:(cc + 1) * 128], ident, [t_src, t_c], [tb])
                    wd = t_dst if half == 0 else A(t_dst[1] if isinstance(t_dst, tuple) else t_dst)
                    if evac_eng == "act":
                        act(dst3[:, half * 4:half * 4 + 4, :], v[:, :, 0:npart], AF.Copy, [tb], [wd])
                    else:
                        cp(evac_eng, dst3[:, half * 4:half * 4 + 4, :], v[:, :, 0:npart], [tb], [wd])
            else:
                bk, tb = banks.next()
                v = bk.bitcast(BF16).rearrange("p (c t) -> p c t", c=8)
                for c in range(8):
                    tr(v[:, c, 0:npart], src_tok_ap[:, c * 128:(c + 1) * 128], ident, [t_src, t_c], [tb])
                if evac_eng == "act":
                    act(dst3, v[:, :, 0:npart], AF.Copy, [tb], [t_dst])
                else:
                    cp(evac_eng, dst3, v[:, :, 0:npart], [tb], [t_dst])

        def norm_blocks(st, tag, xg, t_xg, nb, npart, dst_fn, t_dst, junk, t_junk, xs_rot):
            ssq = sb(st, f"ssq_{tag}", [128, nb], F32)
            t_ss = P.tok()
            for b in range(nb):
                stt("dve", junk[0:npart, :], xg[0:npart, b, :], 1.0, xg[0:npart, b, :], ALU.mult, ALU.mult,
                    [t_xg], [t_ss if b == 0 else A(t_ss)], accum_out=ssq[0:npart, b:b + 1])
            rstd_from_sum(ssq[0:npart, :], ssq[0:npart, :], D, [t_ss], t_ss)
            for b in range(nb):
                xs, t_xs = xs_rot.next()
                ts("dve", xs[0:npart, :], xg[0:npart, b, :], ssq[0:npart, b:b + 1], None, ALU.mult, None,
                   [t_xg, t_ss], [t_xs])
                transpose_to(dst_fn(b), xs[0:npart, :], npart, identb[0:npart, 0:npart], t_xs, t_dst if b == 0 else A(t_dst))
            return ssq, t_ss

        def load_w(st, dst3, src, nch, ncols, gain_col0=None, prow=128, stage_rot=None, t_w=None, queue="pool"):
            if gain_col0 is None:
                for c in range(nch):
                    P.dma(queue, dst3[:, c, :], src[c * prow:(c + 1) * prow, :], (), [A(t_w)])
            else:
                for c in range(nch):
                    for c0 in range(0, ncols, 2048):
                        c1 = min(ncols, c0 + 2048)
                        stg, t_s = stage_rot.next()
                        P.dma("sp", stg[0:prow, 0:c1 - c0], src[c * prow:(c + 1) * prow, c0:c1], (), [t_s])
                        act(dst3[:, c, c0:c1], stg[0:prow, 0:c1 - c0], AF.Copy, [t_s, t_c], [A(t_w)],
                            scale=vecs[0:prow, gain_col0 + c:gain_col0 + c + 1])

        def rope_tables(st, tag, pos_d, ntok, dst):
            L = ntok // 8
            posi = sb(st, f"posi_{tag}", [128, L], I32)
            ang = sb(st, f"ang_{tag}", [128, L], F32)
            kf = sb(st, f"kf_{tag}", [128, L], F32)
            ki = sb(st, f"ki_{tag}", [128, L], I32)
            r = sb(st, f"r_{tag}", [128, L], F32)
            y = sb(st, f"y_{tag}", [128, L], F32)
            m = sb(st, f"m_{tag}", [128, L], F32)
            t_p, t_a, t_k, t_r, t_y, t_m = (P.tok() for _ in range(6))
            for g in range(8):
                P.dma("sp", posi[16 * g:16 * g + 16, :], pos_d[0:1, g * L:(g + 1) * L].broadcast_to([16, L]), (), [A(t_p)])
            cp("dve", ang[:], posi[:], [t_p], [t_a])
            ts("dve", ang[:], ang[:], vecs[:, V_INVF:V_INVF + 1], None, ALU.mult, None, [t_a, t_c], [t_a])
            ts("dve", kf[:], ang[:], 1.0 / (2 * math.pi), None, ALU.mult, None, [t_a], [t_k])
            cp("dve", ki[:], kf[:], [t_k], [t_k])
            cp("dve", kf[:], ki[:], [t_k], [t_k])
            stt("dve", r[:], kf[:], -2 * math.pi, ang[:], ALU.mult, ALU.add, [t_k, t_a], [t_r])
            ts("dve", y[:], r[:], 0.5 * math.pi, None, ALU.add, None, [t_r], [t_y])
            ts("dve", m[:], y[:], math.pi, -2 * math.pi, ALU.is_gt, ALU.mult, [t_y], [t_m])
            tt("dve", y[:], y[:], m[:], ALU.add, [t_y, t_m], [t_y])
            ts("dve", y[:], y[:], 3.141592, -3.141592, ALU.min, ALU.max, [t_y], [t_y])
            act(y[:], y[:], AF.Sin, [t_y], [t_y])
            ts("dve", r[:], r[:], 3.141592, -3.141592, ALU.min, ALU.max, [t_r], [t_r])
            act(r[:], r[:], AF.Sin, [t_r], [t_r])
            ts("dve", m[:], r[:], -1.0, None, ALU.mult, None, [t_r, t_m], [t_m])
            t_out = P.tok()
            for g in range(8):
                P.dma("sp", dst[0, :, g * L:(g + 1) * L], y[16 * g:16 * g + 16, :], [t_y], [A(t_out)])
                P.dma("sp", dst[1, :, g * L:(g + 1) * L], r[16 * g:16 * g + 16, :], [t_r], [A(t_out)])
                P.dma("sp", dst[2, :, g * L:(g + 1) * L], m[16 * g:16 * g + 16, :], [t_m], [A(t_out)])
            return t_out

        def load_tab(cosd, sind, tab, t_tab, c0, c1, t_dst):
            n = c1 - c0
            P.dma("sp", cosd[64:80, 0:n], tab[0, :, c0:c1], [t_tab], [t_dst])
            P.dma("sp", cosd[80:96, 0:n], tab[0, :, c0:c1], [t_tab], [A(t_dst)])
            P.dma("sp", sind[64:80, 0:n], tab[2, :, c0:c1], [t_tab], [A(t_dst)])
            P.dma("sp", sind[80:96, 0:n], tab[1, :, c0:c1], [t_tab], [A(t_dst)])

        with contextlib.ExitStack() as stA:
            attnT = sb(stA, "attnT", [64, 8, T], BF16)
            t_attn = [P.tok() for _ in range(NP * 8)]
            with contextlib.ExitStack() as stQ:
                QT = sb(stQ, "QT", [96, 8, T], BF16)
                t_QT = P.tok()
                with contextlib.ExitStack() as stR:
                    t_tab_all = rope_tables(stR, "a", pos_all, S, tab_all)
                    P.barrier()
                with contextlib.ExitStack() as stR:
                    t_tab_own = rope_tables(stR, "o", pos_own, T, tab_own)
                    P.barrier()

                with contextlib.ExitStack() as st1:
                    Wq = sb(st1, "Wq", [128, 8, 384], BF16)
                    Wkv = sb(st1, "Wkv", [128, 8, 256], BF16)
                    WkrA = sb(st1, "WkrA", [128, 8, 96], BF16)
                    WkrB = sb(st1, "WkrB", [128, 8, 96], BF16)
                    WuqA = sb(st1, "WuqA", [128, 3, 768], BF16)
                    WuqB = sb(st1, "WuqB", [128, 3, 768], BF16)
                    Wuk = sb(st1, "Wuk", [128, 2, 512], BF16)
                    Wuv = sb(st1, "Wuv", [128, 2, 512], BF16)
                    t_w1 = P.tok()
                    with contextlib.ExitStack() as stS:
                        stage = Rot([(sb(stS, f"stage{i}", [128, 2048], F32), P.tok()) for i in range(2)])
                        load_w(stS, Wq, w_q, 8, 384, V_GPM, stage_rot=stage, t_w=t_w1)
                        load_w(stS, Wkv, w_kv, 8, 256, V_GPM, stage_rot=stage, t_w=t_w1)
                        load_w(stS, WkrA, w_krA, 8, 96, V_GPM, stage_rot=stage, t_w=t_w1)
                        load_w(stS, WkrB, w_krB, 8, 96, V_GPM, stage_rot=stage, t_w=t_w1)
                        load_w(stS, WuqA, w_uqA, 3, 768, V_GQ, stage_rot=stage, t_w=t_w1)
                        load_w(stS, WuqB, w_uqB, 3, 768, V_GQ, stage_rot=stage, t_w=t_w1)
                        load_w(stS, Wuk, w_uk, 2, 512, V_GKV, stage_rot=stage, t_w=t_w1)
                        load_w(stS, Wuv, w_uv, 2, 512, V_GKV, stage_rot=stage, t_w=t_w1)
                        P.barrier()

                    junk = sb(st1, "junk", [128, 1024], BF16)
                    t_junk = P.tok()
                    xs_rot = Rot([(sb(st1, f"xs{i}", [128, 1024], BF16), P.tok()) for i in range(2)])
                    cosd = Rot([(sb(st1, f"cosd{i}", [96, 512], F32), sb(st1, f"sind{i}", [96, 512], F32), P.tok())
                                for i in range(2)])
                    t1r = Rot([(sb(st1, f"t1r{i}", [96, 512], F32), P.tok()) for i in range(2)])
                    t2r = Rot([(sb(st1, f"t2r{i}", [96, 512], F32), P.tok()) for i in range(2)])
                    rsb = Rot([(sb(st1, f"rsb{i}", [128, 512], F32), P.tok()) for i in range(2)])
                    sqr = Rot([(sb(st1, f"sqr{i}", [128, 3, 512], BF16), P.tok()) for i in range(2)])
                    xg_rot = Rot([(sb(st1, f"xg_{i}", [128, 4, 1024], F32), P.tok()) for i in range(2)])
                    aT_rot = Rot([(sb(st1, f"aT_{i}", [128, 8, 512], BF16), P.tok()) for i in range(2)])

                    with contextlib.ExitStack() as st:
                        qd_rot = Rot([(sb(st, f"qd{i}", [128, 3, 512], BF16), P.tok()) for i in range(2)])
                        for ti in range(NJ // 4):
                            c0 = ti * 512
                            xg, t_xg = xg_rot.next()
                            P.dma("sp", xg[:], x_own[c0:c0 + 512, :].rearrange("(b p) d -> p b d", p=128), (), [t_xg])
                            aT, t_aT = aT_rot.next()
                            norm_blocks(st, f"p0_{ti}", xg, t_xg, 4, 128, lambda b: aT[:, :, b * 128:(b + 1) * 128],
                                        t_aT, junk, t_junk, xs_rot)
                            qd, t_qd = qd_rot.next()
                            sq, t_sq = sqr.next()
                            for c in range(3):
                                bk, tb = banks.next()
                                for d in range(8):
                                    mm(bk, Wq[:, d, c * 128:(c + 1) * 128], aT[:, d, :], d == 0, d == 7, [t_w1, t_aT], [tb])
                                act(qd[:, c, :], bk, AF.Copy, [tb], [t_qd if c == 0 else A(t_qd)])
                                act(sq[:, c, :], bk, AF.Square, [tb], [t_sq if c == 0 else A(t_sq)])
                            bk, tb = banks.next()
                            for c in range(3):
                                mm(bk, ones_bf[:], sq[:, c, :], c == 0, c == 2, [t_c, t_sq], [tb])
                            rs, t_rs = rsb.next()
                            rstd_from_sum(rs[:], bk, 384, [tb], t_rs)
                            cs, sn, t_cs = cosd.next()
                            load_tab(cs, sn, tab_own, t_tab_own, c0, c0 + 512, t_cs)
                            tt("dve", cs[64:96, :], cs[64:96, :], rs[64:96, :], ALU.mult, [t_cs, t_rs], [t_cs])
                            tt("dve", sn[64:96, :], sn[64:96, :], rs[64:96, :], ALU.mult, [t_cs, t_rs], [t_cs])
                            for h in range(8):
                                bkA, tbA = banks.next()
                                bkB, tbB = banks.next()
                                for c in range(3):
                                    mm(bkA[0:96, :], WuqA[:, c, h * 96:(h + 1) * 96], qd[:, c, :], c == 0, c == 2,
                                       [t_w1, t_qd], [tbA])
                                for c in range(3):
                                    mm(bkB[0:96, :], WuqB[:, c, h * 96:(h + 1) * 96], qd[:, c, :], c == 0, c == 2,
                                       [t_w1, t_qd], [tbB])
                                tt("dve", QT[0:64, h, c0:c0 + 512], bkA[0:64, :], rs[0:64, :], ALU.mult, [tbA, t_rs], [A(t_QT)])
                                t1, t_t1 = t1r.next()
                                t2, t_t2 = t2r.next()
                                tt("dve", t1[64:96, :], bkA[64:96, :], cs[64:96, :], ALU.mult, [tbA, t_cs], [t_t1])
                                tt("dve", t2[64:96, :], bkB[64:96, :], sn[64:96, :], ALU.mult, [tbB, t_cs], [t_t2])
                                tt("pool", QT[64:96, h, c0:c0 + 512], t1[64:96, :], t2[64:96, :], ALU.add,
                                   [t_t1, t_t2], [A(t_QT)])
                        if debug:
                            P.dma("sp", qt_dbg, QT[:].rearrange("p h t -> p (h t)"), [t_QT], [])
                        P.barrier()

                    with contextlib.ExitStack() as st:
                        kvd_rot = Rot([(sb(st, f"kvd{i}", [128, 2, 512], BF16), P.tok()) for i in range(2)])
                        KTn_rot = Rot([(sb(st, f"KTn{i}", [128, 4, 512], BF16), P.tok()) for i in range(2)])
                        krT_rot = Rot([(sb(st, f"krT{i}", [96, 512], BF16), P.tok()) for i in range(2)])
                        Va_rot = Rot([(sb(st, f"Va{i}", [128, 8, 4, 65], BF16), P.tok()) for i in range(2)])
                        rcol_rot = Rot([(sb(st, f"rcol{i}", [128, 4], F32), P.tok()) for i in range(2)])
                        for (va, t_va) in Va_rot.items:
                            memset("pool", va[:].rearrange("p a b e -> p (a b e)"), 1.0, [t_va])
                        t_kscr = P.tok()
                        t_vscr = P.tok()
                        for hu in range(2 * R):
                            tok0 = hu * 512
                            xg, t_xg = xg_rot.next()
                            P.dma("sp", xg[:], x_all[tok0:tok0 + 512, :].rearrange("(b p) d -> p b d", p=128), (), [t_xg])
                            aT, t_aT = aT_rot.next()
                            norm_blocks(st, f"p1_{hu}", xg, t_xg, 4, 128, lambda b: aT[:, :, b * 128:(b + 1) * 128],
                                        t_aT, junk, t_junk, xs_rot)
                            KTn, t_KTn = KTn_rot.next()
                            krT, t_krT = krT_rot.next()
                            Va, t_Va = Va_rot.next()
                            kvd, t_kvd = kvd_rot.next()
                            sq, t_sq = sqr.next()
                            for c in range(2):
                                bk, tb = banks.next()
                                for d in range(8):
                                    mm(bk, Wkv[:, d, c * 128:(c + 1) * 128], aT[:, d, :], d == 0, d == 7, [t_w1, t_aT], [tb])
                                act(kvd[:, c, :], bk, AF.Copy, [tb], [t_kvd if c == 0 else A(t_kvd)])
                                act(sq[:, c, :], bk, AF.Square, [tb], [t_sq if c == 0 else A(t_sq)])
                            bkA, tbA = banks.next()
                            bkB, tbB = banks.next()
                            for d in range(8):
                                mm(bkA[0:96, :], WkrA[:, d, :], aT[:, d, :], d == 0, d == 7, [t_w1, t_aT], [tbA])
                            for d in range(8):
                                mm(bkB[0:96, :], WkrB[:, d, :], aT[:, d, :], d == 0, d == 7, [t_w1, t_aT], [tbB])
                            cs, sn, t_cs = cosd.next()
                            load_tab(cs, sn, tab_all, t_tab_all, tok0, tok0 + 512, t_cs)
                            t1, t_t1 = t1r.next()
                            t2, t_t2 = t2r.next()
                            tt("dve", t1[64:96, :], bkA[64:96, :], cs[64:96, :], ALU.mult, [tbA, t_cs], [t_t1])
                            tt("dve", t2[64:96, :], bkB[64:96, :], sn[64:96, :], ALU.mult, [tbB, t_cs], [t_t2])
                            tt("pool", krT[64:96, :], t1[64:96, :], t2[64:96, :], ALU.add, [t_t1, t_t2], [t_krT])
                            bk, tb = banks.next()
                            for c in range(2):
                                mm(bk, ones_bf[:], sq[:, c, :], c == 0, c == 1, [t_c, t_sq], [tb])
                            rs, t_rs = rsb.next()
                            rstd_from_sum(rs[:], bk, 256, [tb], t_rs)
                            bk2, tb2 = banks.next()
                            for b in range(4):
                                for c in range(2):
                                    mm(bk2[:, b:b + 1], sq[:, c, b * 128:(b + 1) * 128], ones_bf[:, 0:1], c == 0, c == 1,
                                       [t_c, t_sq], [tb2])
                            rc, t_rc = rcol_rot.next()
                            rstd_from_sum(rc[:], bk2[:, 0:4], 256, [tb2], t_rc)
                            for hp in range(4):
                                bk, tb = banks.next()
                                for c in range(2):
                                    mm(bk, Wuk[:, c, hp * 128:(hp + 1) * 128], kvd[:, c, :], c == 0, c == 1, [t_w1, t_kvd], [tb])
                                tt("dve", KTn[:, hp, :], bk, rs[:], ALU.mult, [tb, t_rs], [t_KTn if hp == 0 else A(t_KTn)])
                            for b in range(4):
                                bk, tb = banks.next()
                                for c in range(2):
                                    mm(bk, kvd[:, c, b * 128:(b + 1) * 128], Wuv[:, c, :], c == 0, c == 1, [t_w1, t_kvd], [tb])
                                act(Va[:, :, b, 0:64], bk.rearrange("p (h d) -> p h d", h=8), AF.Copy,
                                    [tb, t_rc], [t_Va if b == 0 else A(t_Va)], scale=rc[:, b:b + 1])
                            for h in range(8):
                                P.dma("sp", kscr[h, 0:64, tok0:tok0 + 512],
                                      KTn[(h % 2) * 64:(h % 2) * 64 + 64, h // 2, :], [t_KTn], [A(t_kscr)])
                                P.dma("sp", kscr[h, 64:96, tok0:tok0 + 512], krT[64:96, :], [t_krT], [A(t_kscr)])
                                P.dma("sp", vscr[h, :, hu * 4 * 65:(hu + 1) * 4 * 65],
                                      Va[:, h, :, :].rearrange("p b e -> p (b e)"), [t_Va], [A(t_vscr)])
                        P.barrier()

                with contextlib.ExitStack() as st:
                    KT_rot = Rot([(sb(st, f"KT{i}", [96, S], BF16), sb(st, f"VH{i}", [128, NB, 65], BF16), P.tok())
                                  for i in range(2)])
                    maskb = sb(st, "maskb_sb", [128, 8, 128], BF16)
                    t_mask = P.tok()
                    P.dma("pool", maskb[:].rearrange("p i q -> p (i q)"), maskb_d, (), [t_mask])
                    PT_rot = Rot([(sb(st, f"PT{i}", [128, 2, 512], BF16), P.tok()) for i in range(3)])
                    osb_rot = Rot([(sb(st, f"osb{i}", [65, 512], F32), P.tok()) for i in range(2)])
                    rec_rot = Rot([(sb(st, f"rec{i}", [64, 512], F32), P.tok()) for i in range(2)])
                    S_rot = Rot([(pp[0], P.tok()), (pp[1], P.tok())])
                    O_rot = Rot([(pp[2][:, 0, :], P.tok()), (pp[2][:, 1, :], P.tok())])
                    bc_bank, t_bc = pp[3][:, 0, :], P.tok()
                    scale = 96.0 ** -0.5
                    pending = None

                    def flush(pend):
                        PT, t_PT, O, t_O, VH, t_KV, kbs, W, qo, nkb, fin = pend
                        for u, kb in enumerate(kbs):
                            mm(O[0:65, qo:512], VH[:, kb, :], PT[:, u, 0:W], kb == 0, kb == nkb - 1, [t_KV, t_PT], [t_O])
                        if fin is not None:
                            fin()

                    for h in range(8):
                        KT, VH, t_KV = KT_rot.next()
                        P.dma("sp", KT[:], kscr[h], [t_kscr], [t_KV])
                        P.dma("sp", VH[:].rearrange("p b e -> p (b e)"), vscr[h], [t_vscr], [A(t_KV)])
                        for p in range(NP):
                            O, t_O = O_rot.next()
                            nkb = 32 * p + 32
                            q0 = p * 512

                            def make_fin(h=h, p=p, O=O, t_O=t_O, q0=q0):
                                def fin():
                                    osb, t_osb = osb_rot.next()
                                    cp("dve", osb[:], O[0:65, :], [t_O], [t_osb])
                                    mm(bc_bank[0:64, :], sel[:], osb[:], True, True, [t_c, t_osb], [t_bc])
                                    rec, t_rec = rec_rot.next()
                                    recip(rec[:], bc_bank[0:64, :], [t_bc], [t_rec])
                                    tt("dve", attnT[:, h, q0:q0 + 512], osb[0:64, :], rec[:], ALU.mult, [t_osb, t_rec],
                                       [t_attn[p * 8 + h]])
                                return fin

                            for kb in range(0, nkb, 2):
                                if kb < 32 * p:
                                    W, diag = 512, False
                                else:
                                    jj = (kb - 32 * p) // 8
                                    W, diag = (4 - jj) * 128, True
                                qo = 512 - W
                                Sx, t_S = S_rot.next()
                                for u in range(2):
                                    kbu = kb + u
                                    mm(Sx[:, u, 0:W], KT[:, kbu * 128:(kbu + 1) * 128], QT[:, h, q0 + qo:q0 + 512], True, not diag,
                                       [t_KV, t_QT], [t_S])
                                    if diag:
                                        mm(Sx[:, u, 0:128], identb[:], maskb[:, kbu % 8, :], False, True, [t_c, t_mask], [t_S])
                                if pending is not None:
                                    flush(pending)
                                PT, t_PT = PT_rot.next()
                                act(PT[:, :, 0:W], Sx[:, :, 0:W], AF.Exp, [t_S], [t_PT], scale=scale)
                                pending = (PT, t_PT, O, t_O, VH, t_KV, (kb, kb + 1), W, qo, nkb,
                                           make_fin() if kb + 2 >= nkb else None)
                    flush(pending)
                    if debug:
                        P.dma("sp", at_dbg, attnT[:].rearrange("p h t -> p (h t)"), t_attn, [])
                    P.barrier()
            with contextlib.ExitStack() as st:
                Wpin = sb(st, "Wpin", [128, 8, 512], BF16)
                Wg = sb(st, "Wg", [128, 8, 2048], BF16)
                Wba = sb(st, "Wba", [64, 8, 1024], BF16)
                Wbp = sb(st, "Wbp", [128, 4, 1024], BF16)
                Wout = sb(st, "Wout", [128, 8, 1024], BF16)
                Wpl = sb(st, "Wpl", [128, 4, 128], BF16)
                band0 = sb(st, "band0_sb", [128, 4, 128], BF16)
                bandN = sb(st, "bandN_sb", [128, 4, 128], BF16)
                bandH = sb(st, "bandH_sb", [16, 4, 128], BF16)
                t_w2 = P.tok()
                with contextlib.ExitStack() as stS:
                    stage = Rot([(sb(stS, f"stageb{i}", [128, 2048], F32), P.tok()) for i in range(2)])
                    load_w(stS, Wpin, w_pin, 8, 512, V_GPM, stage_rot=stage, t_w=t_w2)
                    load_w(stS, Wg, w_gate, 8, 2048, V_GPM, stage_rot=stage, t_w=t_w2)
                    load_w(stS, Wba, w_ba, 8, 1024, prow=64, t_w=t_w2)
                    load_w(stS, Wbp, w_bp, 4, 1024, t_w=t_w2)
                    load_w(stS, Wout, w_out, 8, 1024, t_w=t_w2)
                    load_w(stS, Wpl, w_pool, 4, 128, t_w=t_w2)
                    P.dma("pool", band0[:].rearrange("p g t -> p (g t)"), band0_d, (), [A(t_w2)])
                    P.dma("pool", bandN[:].rearrange("p g t -> p (g t)"), bandN_d, (), [A(t_w2)])
                    P.dma("pool", bandH[:].rearrange("p g t -> p (g t)"), bandH_d, (), [A(t_w2)])
                    P.barrier()
                junk = sb(st, "junk2", [128, 1024], BF16)
                t_junk = P.tok()
                xs_rot = Rot([(sb(st, f"xsb{i}", [128, 1024], BF16), P.tok()) for i in range(2)])
                xg_rot = Rot([(sb(st, f"xg2_{i}", [128, 2, 1024], F32), P.tok()) for i in range(1)])
                xh_rot = Rot([(sb(st, f"xh2_{i}", [32, 1, 1024], F32), P.tok()) for i in range(1)])
                aT_rot = Rot([(sb(st, f"aT2_{i}", [128, 8, 256], BF16), P.tok()) for i in range(1)])
                aTh_rot = Rot([(sb(st, f"aTh_{i}", [128, 8, 32], BF16), P.tok()) for i in range(1)])
                xT_rot = Rot([(sb(st, f"xT_{i}", [128, 8, 256], F32), P.tok()) for i in range(1)])
                u_rot = Rot([(sb(st, f"u_{i}", [128, 512], BF16), P.tok()) for i in range(2)])
                uh_rot = Rot([(sb(st, f"uh_{i}", [16, 512], BF16), P.tok()) for i in range(2)])
                dT_rot = Rot([(sb(st, f"dT_{i}", [128, 4, 256], BF16), P.tok()) for i in range(1)])
                pl_rot = Rot([(sb(st, f"pl_{i}", [128, 4, 256], BF16), P.tok()) for i in range(1)])
                gate_rot = Rot([(sb(st, f"gate_{i}", [128, 16, 256], BF16), P.tok()) for i in range(1)])
                mg_rot = Rot([(sb(st, f"mg_{i}", [128, 8, 256], BF16), P.tok()) for i in range(1)])
                tmp_rot = Rot([(sb(st, f"tmpa_{i}", [128, 256], F32), P.tok()) for i in range(4)])
                y_rot = Rot([(sb(st, f"y_{i}", [128, 8, 256], F32), P.tok()) for i in range(1)])
                sq_rot = Rot([(sb(st, f"sq8_{i}", [128, 8, 256], BF16), P.tok()) for i in range(1)])
                rs_rot = Rot([(sb(st, f"rs2_{i}", [128, 256], F32), P.tok()) for i in range(2)])
                h1_rot = Rot([(sb(st, f"h1_{i}", [128, 8, 256], F32), P.tok()) for i in range(1)])
                t_hscr = P.tok()
                for ti in range(T // 256):
                    c0 = ti * 256
                    xg, t_xg = xg_rot.next()
                    P.dma("sp", xg[:], x_own[c0:c0 + 256, :].rearrange("(b p) d -> p b d", p=128), (), [t_xg])
                    xh, t_xh = xh_rot.next()
                    P.dma("sp", xh[:, 0, :], x_halo[ti * 32:(ti + 1) * 32, :], (), [t_xh])
                    aT, t_aT = aT_rot.next()
                    norm_blocks(st, f"s1_{ti}", xg, t_xg, 2, 128, lambda b: aT[:, :, b * 128:(b + 1) * 128],
                                t_aT, junk, t_junk, xs_rot)
                    aTh, t_aTh = aTh_rot.next()
                    norm_blocks(st, f"s1h_{ti}", xh, t_xh, 1, 32, lambda b: aTh[:, :, :], t_aTh, junk, t_junk, xs_rot)
                    xT, t_xT = xT_rot.next()
                    for b in range(2):
                        transpose_to(xT[:, :, b * 128:(b + 1) * 128], xg[:, b, :], 128, identf[:], t_xg,
                                     t_xT if b == 0 else A(t_xT), f32=True)
                    dT, t_dT = dT_rot.next()
                    for b in range(2):
                        bk, tb = banks.next()
                        for d in range(8):
                            mm(bk, aT[:, d, b * 128:(b + 1) * 128], Wpin[:, d, :], d == 0, d == 7, [t_aT, t_w2], [tb])
                        u, t_u = u_rot.next()
                        act(u[:], bk, AF.Copy, [tb], [t_u])
                        bk, tb = banks.next()
                        for d in range(8):
                            mm(bk[0:16, :], aTh[:, d, b * 16:(b + 1) * 16], Wpin[:, d, :], d == 0, d == 7, [t_aTh, t_w2], [tb])
                        uh, t_uh = uh_rot.next()
                        act(uh[:], bk[0:16, :], AF.Copy, [tb], [t_uh])
                        band = band0 if (ti == 0 and b == 0) else bandN
                        bk, tb = banks.next()
                        for g in range(4):
                            mm(bk[:, g * 128:(g + 1) * 128], u[:, g * 128:(g + 1) * 128], band[:, g, :], True, False,
                               [t_u, t_w2], [tb])
                            mm(bk[:, g * 128:(g + 1) * 128], uh[:, g * 128:(g + 1) * 128], bandH[:, g, :], False, True,
                               [t_uh, t_w2], [tb])
                        act(dT[:, :, b * 128:(b + 1) * 128], bk.rearrange("p (g t) -> p g t", g=4), AF.Copy, [tb],
                            [t_dT if b == 0 else A(t_dT)])
                    pl, t_pl = pl_rot.next()
                    for g in range(4):
                        bk, tb = banks.next()
                        mm(bk[:, 0:256], Wpl[:, g, :], dT[:, g, :], True, True, [t_w2, t_dT], [tb])
                        act(pl[:, g, :], bk[:, 0:256], AF.Copy, [tb, t_c], [t_pl if g == 0 else A(t_pl)],
                            scale=vecs[:, V_PSC + g:V_PSC + g + 1])
                    gate, t_gate = gate_rot.next()
                    for oc in range(16):
                        bk, tb = banks.next()
                        for d in range(8):
                            mm(bk[:, 0:256], Wg[:, d, oc * 128:(oc + 1) * 128], aT[:, d, :], d == 0, d == 7, [t_w2, t_aT], [tb])
                        act(gate[:, oc, :], bk[:, 0:256], AF.Sigmoid, [tb, t_c], [t_gate if oc == 0 else A(t_gate)],
                            bias=vecs[:, V_BG + oc:V_BG + oc + 1])
                    mg, t_mg = mg_rot.next()
                    t_at_tile = [t_attn[(c0 // 512) * 8 + h] for h in range(8)]
                    for oc in range(8):
                        bkA, tbA = banks.next()
                        for h in range(8):
                            mm(bkA[:, 0:256], Wba[:, h, oc * 128:(oc + 1) * 128], attnT[:, h, c0:c0 + 256], h == 0, h == 7,
                               [t_w2, t_at_tile[h]], [tbA])
                        bkB, tbB = banks.next()
                        for g in range(4):
                            mm(bkB[:, 0:256], Wbp[:, g, oc * 128:(oc + 1) * 128], pl[:, g, :], g == 0, g == 3, [t_w2, t_pl], [tbB])
                        ta, t_ta = tmp_rot.next()
                        tt("dve", ta[:], bkA[:, 0:256], gate[:, oc, :], ALU.mult, [tbA, t_gate], [t_ta])
                        tb_, t_tb = tmp_rot.next()
                        tt("dve", tb_[:], bkB[:, 0:256], gate[:, 8 + oc, :], ALU.mult, [tbB, t_gate], [t_tb])
                        tt("pool", mg[:, oc, :], ta[:], tb_[:], ALU.add, [t_ta, t_tb], [t_mg if oc == 0 else A(t_mg)])
                    y, t_y = y_rot.next()
                    sq, t_sq = sq_rot.next()
                    for oc in range(8):
                        bk, tb = banks.next()
                        for kc in range(8):
                            mm(bk[:, 0:256], Wout[:, kc, oc * 128:(oc + 1) * 128], mg[:, kc, :], kc == 0, kc == 7, [t_w2, t_mg], [tb])
                        act(y[:, oc, :], bk[:, 0:256], AF.Copy, [tb], [t_y if oc == 0 else A(t_y)])
                        act(sq[:, oc, :], bk[:, 0:256], AF.Square, [tb], [t_sq if oc == 0 else A(t_sq)])
                    bk, tb = banks.next()
                    for oc in range(8):
                        mm(bk[:, 0:256], ones_bf[:], sq[:, oc, :], oc == 0, oc == 7, [t_c, t_sq], [tb])
                    rs, t_rs = rs_rot.next()
                    rstd_from_sum(rs[:], bk[:, 0:256], D, [tb], t_rs)
                    h1, t_h1 = h1_rot.next()
                    for oc in range(8):
                        stt("dve", y[:, oc, :], y[:, oc, :], vecs[:, V_GPOM + oc:V_GPOM + oc + 1], rs[:], ALU.mult, ALU.mult,
                            [t_y, t_rs, t_c], [A(t_y)])
                        tt("pool", h1[:, oc, :], y[:, oc, :], xT[:, oc, :], ALU.add, [t_y, t_xT], [t_h1 if oc == 0 else A(t_h1)])
                    P.dma("sp", hscr.rearrange("p (c t) -> p c t", c=8)[:, :, c0:c0 + 256], h1[:], [t_h1], [A(t_hscr)])
                P.barrier()
        with contextlib.ExitStack() as st:
            W1 = sb(st, "W1", [128, 8, 4096], BF16)
            W2 = sb(st, "W2", [128, 32, 1024], BF16)
            Wpe = sb(st, "Wpe", [128, 2, 1024], BF16)
            Wpg = sb(st, "Wpg", [128, 8, 1024], BF16)
            t_w3 = P.tok()
            with contextlib.ExitStack() as stS:
                stage = Rot([(sb(stS, f"stagec{i}", [128, 2048], F32), P.tok()) for i in range(2)])
                load_w(stS, W1, w_ff1, 8, 4096, V_GPRM, stage_rot=stage, t_w=t_w3)
                load_w(stS, W2, w_ff2, 32, 1024, t_w=t_w3)
                load_w(stS, Wpe, w_pe, 2, 1024, t_w=t_w3)
                load_w(stS, Wpg, w_pg, 8, 1024, t_w=t_w3)
                P.barrier()
            h_rot = Rot([(sb(st, f"h_{i}", [128, 8, 256], F32), P.tok()) for i in range(1)])
            sq_rot = Rot([(sb(st, f"sq9_{i}", [128, 8, 256], BF16), P.tok()) for i in range(1)])
            rs_rot = Rot([(sb(st, f"rs3_{i}", [128, 256], F32), P.tok()) for i in range(2)])
            m_rot = Rot([(sb(st, f"m_{i}", [128, 8, 256], BF16), P.tok()) for i in range(1)])
            rl_rot = Rot([(sb(st, f"rl_{i}", [128, 2, 256], BF16), P.tok()) for i in range(2)])
            hh_rot = Rot([(sb(st, f"hh_{i}", [128, 8, 256], BF16), P.tok()) for i in range(2)])
            f_rot = Rot([(sb(st, f"f_{i}", [128, 8, 256], F32), P.tok()) for i in range(1)])
            pt_rot = Rot([(sb(st, f"pt_{i}", [128, 2, 256], F32), P.tok()) for i in range(1)])
            pT_rot = Rot([(sb(st, f"pT_{i}", [128, 2, 256], BF16), P.tok()) for i in range(1)])
            pg_rot = Rot([(sb(st, f"pg_{i}", [128, 256], F32), P.tok()) for i in range(2)])
            ot_rot = Rot([(sb(st, f"ot_{i}", [128, 1024], F32), P.tok()) for i in range(1)])
            t_out = P.tok()
            gen_banks = Rot([(pp[i][:, j, :], P.tok()) for i in range(2) for j in range(2)])
            y2_banks = [(pp[2 + i][:, j, :], P.tok()) for i in range(2) for j in range(2)]

            def post_norm(src, t_src, sq, t_sq, hcur, t_h, gcol):
                bk, tb = gen_banks.next()
                for oc in range(8):
                    mm(bk[:, 0:256], ones_bf[:], sq[:, oc, :], oc == 0, oc == 7, [t_c, t_sq], [tb])
                rs, t_rs = rs_rot.next()
                rstd_from_sum(rs[:], bk[:, 0:256], D, [tb], t_rs)
                for oc in range(8):
                    stt("dve", src[:, oc, :], src[:, oc, :], vecs[:, gcol + oc:gcol + oc + 1], rs[:], ALU.mult, ALU.mult,
                        [t_src, t_rs, t_c], [A(t_src)])
                    tt("pool", hcur[:, oc, :], hcur[:, oc, :], src[:, oc, :], ALU.add, [t_src, t_h], [A(t_h)])

            for ti in range(T // 256):
                c0 = ti * 256
                hc, t_h = h_rot.next()
                P.dma("sp", hc[:], hscr.rearrange("p (c t) -> p c t", c=8)[:, :, c0:c0 + 256], [t_hscr], [t_h])
                pt, t_pt = pt_rot.next()
                P.dma("sp", pt[:], p_own[c0:c0 + 256, :].rearrange("(b p) d -> p b d", p=128), (), [t_pt])
                sq, t_sq = sq_rot.next()
                act(sq[:], hc[:], AF.Square, [t_h], [t_sq])
                bk, tb = gen_banks.next()
                for oc in range(8):
                    mm(bk[:, 0:256], ones_bf[:], sq[:, oc, :], oc == 0, oc == 7, [t_c, t_sq], [tb])
                rs, t_rs = rs_rot.next()
                rstd_from_sum(rs[:], bk[:, 0:256], D, [tb], t_rs)
                m, t_m = m_rot.next()
                for oc in range(8):
                    tt("dve", m[:, oc, :], hc[:, oc, :], rs[:], ALU.mult, [t_h, t_rs], [t_m if oc == 0 else A(t_m)])
                for fg in range(4):
                    hh, t_hh = hh_rot.next()
                    for fp in range(4):
                        bk, tb = gen_banks.next()
                        for u in range(2):
                            fc = fg * 8 + fp * 2 + u
                            for d in range(8):
                                mm(bk[:, u * 256:(u + 1) * 256], W1[:, d, fc * 128:(fc + 1) * 128], m[:, d, :], d == 0, d == 7,
                                   [t_w3, t_m], [tb])
                        rl, t_rl = rl_rot.next()
                        act(rl[:], bk.rearrange("p (u t) -> p u t", u=2), AF.Relu, [tb], [t_rl])
                        tt("dve", hh[:, fp * 2:fp * 2 + 2, :], rl[:], rl[:], ALU.mult, [t_rl], [t_hh if fp == 0 else A(t_hh)])
                    for oc in range(8):
                        yb, t_yb = y2_banks[oc // 2]
                        for fl in range(8):
                            fc = fg * 8 + fl
                            mm(yb[:, (oc % 2) * 256:(oc % 2) * 256 + 256], W2[:, fc, oc * 128:(oc + 1) * 128], hh[:, fl, :],
                               fc == 0 and oc % 2 == 0, fc == 31, [t_w3, t_hh], [t_yb])
                f, t_f = f_rot.next()
                sq, t_sq = sq_rot.next()
                for oc in range(8):
                    yb, t_yb = y2_banks[oc // 2]
                    src = yb[:, (oc % 2) * 256:(oc % 2) * 256 + 256]
                    act(f[:, oc, :], src, AF.Copy, [t_yb], [t_f if oc == 0 else A(t_f)])
                    act(sq[:, oc, :], src, AF.Square, [t_yb], [t_sq if oc == 0 else A(t_sq)])
                post_norm(f, t_f, sq, t_sq, hc, t_h, V_GPOL)
                pT, t_pT = pT_rot.next()
                for b in range(2):
                    bk, tb = gen_banks.next()
                    for c in range(2):
                        tr(bk[:, c * 128:(c + 1) * 128], pt[:, b, c * 128:(c + 1) * 128], identf[:], [t_pt, t_c], [tb])
                    act(pT[:, :, b * 128:(b + 1) * 128], bk[:, 0:256].rearrange("p (c t) -> p c t", c=2), AF.Copy, [tb],
                        [t_pT if b == 0 else A(t_pT)])
                hb, t_hb = m_rot.next()
                act(hb[:], hc[:], AF.Copy, [t_h], [t_hb])
                z, t_z = f_rot.next()
                sq, t_sq = sq_rot.next()
                for oc in range(8):
                    bkG, tbG = gen_banks.next()
                    for d in range(8):
                        mm(bkG[:, 0:256], Wpg[:, d, oc * 128:(oc + 1) * 128], hb[:, d, :], d == 0, d == 7, [t_w3, t_hb], [tbG])
                    pg, t_pg = pg_rot.next()
                    act(pg[:], bkG[:, 0:256], AF.Sigmoid, [tbG], [t_pg])
                    bkE, tbE = gen_banks.next()
                    for c in range(2):
                        mm(bkE[:, 0:256], Wpe[:, c, oc * 128:(oc + 1) * 128], pT[:, c, :], c == 0, c == 1, [t_w3, t_pT], [tbE])
                    tt("dve", z[:, oc, :], bkE[:, 0:256], pg[:], ALU.mult, [tbE, t_pg], [t_z if oc == 0 else A(t_z)])
                t_sqz = P.tok()
                act(sq[:], z[:], AF.Square, [t_z], [t_sq])
                post_norm(z, t_z, sq, t_sq, hc, t_h, V_GPLE)
                for b in range(2):
                    ot, t_ot = ot_rot.next()
                    for half in range(2):
                        bk, tb = gen_banks.next()
                        for c in range(4):
                            oc = half * 4 + c
                            tr(bk[:, c * 128:(c + 1) * 128], hc[:, oc, b * 128:(b + 1) * 128], identf[:], [t_h, t_c], [tb])
                        act(ot[:, half * 512:(half + 1) * 512], bk, AF.Copy, [tb], [t_ot if half == 0 else A(t_ot)])
                    P.dma("sp", out_own[c0 + b * 128:c0 + (b + 1) * 128, :], ot[:], [t_ot], [A(t_out)])
            P.barrier()
        P.emit()
    return nc


def _cols(v, n):
    return np.ascontiguousarray(np.asarray(v, np.float32).reshape(n, 128).T)


def pack_inputs(R, x, p, positions, g_pre_mix, w_in, b_gate, g_q, w_uq, g_kv, w_ukv, w_pool, pool_scale,
                w_branch_attn, w_branch_pool, w_out, g_post_mix, g_pre_mlp, w_ff1, w_ff2, g_post_mlp,
                w_ple_proj, w_ple_gate, g_ple):
    NJ = R
    S = 1024 * R
    f32 = np.float32
    x2 = np.asarray(x, f32).reshape(S, D)
    p2 = np.asarray(p, f32).reshape(S, 256)
    pos = np.asarray(positions, np.int32).reshape(S)
    w_in = np.asarray(w_in, f32)[0]
    w_uq = np.asarray(w_uq, f32)[0]
    w_ukv = np.asarray(w_ukv, f32)[0]
    vecs = np.zeros((128, V_N), f32)
    vecs[:, V_GPM:V_GPM + 8] = _cols(g_pre_mix, 8)
    vecs[:, V_GQ:V_GQ + 3] = _cols(g_q, 3)
    vecs[:, V_GKV:V_GKV + 2] = _cols(g_kv, 2)
    vecs[:, V_BG:V_BG + 16] = _cols(b_gate, 16)
    vecs[:, V_PSC:V_PSC + 4] = _cols(pool_scale, 4)
    vecs[:, V_GPOM:V_GPOM + 8] = _cols(g_post_mix, 8)
    vecs[:, V_GPRM:V_GPRM + 8] = _cols(g_pre_mlp, 8)
    vecs[:, V_GPOL:V_GPOL + 8] = _cols(g_post_mlp, 8)
    vecs[:, V_GPLE:V_GPLE + 8] = _cols(g_ple, 8)
    inv_freq = (np.float32(10000.0) ** (-np.arange(0, 32, 2, dtype=np.float32) / np.float32(32))).astype(f32)
    vecs[:, V_INVF] = np.tile(inv_freq, 8)
    wins = (2, 4, 8, 16)
    bandN = np.zeros((128, 4, 128), f32)
    band0 = np.zeros((128, 4, 128), f32)
    bandH = np.zeros((16, 4, 128), f32)
    tt_ = np.arange(128)
    for g, w in enumerate(wins):
        for t in range(128):
            for tp in range(t - w + 1, t + 1):
                if tp >= 0:
                    bandN[tp, g, t] += 1.0 / w
                    band0[tp, g, t] += 1.0 / min(t + 1, w)
                else:
                    bandH[16 + tp, g, t] += 1.0 / w
            bandN[t, g, t] -= 1.0
            band0[t, g, t] -= 1.0
    o_q, o_kv, o_kr, o_pl, o_g = 0, 384, 640, 672, 1184
    w_q = w_in[:, o_q:o_q + 384]
    w_kv = w_in[:, o_kv:o_kv + 256]
    kr = w_in[:, o_kr:o_kr + 32]
    w_krA = np.zeros((D, 96), f32)
    w_krB = np.zeros((D, 96), f32)
    w_krA[:, 64:96] = kr
    w_krB[:, 64:80] = kr[:, 16:32]
    w_krB[:, 80:96] = kr[:, 0:16]
    w_pin = w_in[:, o_pl:o_pl + 512]
    w_gate = w_in[:, o_g:o_g + 2048]
    uq = w_uq.reshape(384, 8, 96)
    w_uqA = uq.copy()
    w_uqB = uq.copy()
    w_uqB[:, :, 64:80] = uq[:, :, 80:96]
    w_uqB[:, :, 80:96] = uq[:, :, 64:80]
    ukv = w_ukv.reshape(256, 8, 128)
    w_uk = ukv[:, :, 0:64].reshape(256, 512)
    w_uv = ukv[:, :, 64:128].reshape(256, 512)
    shared = dict(
        x_all=x2, pos_all=pos.reshape(1, S), vecs=vecs,
        bandN=bandN.reshape(128, 512), bandH=bandH.reshape(16, 512),
        w_q=w_q, w_kv=w_kv, w_krA=w_krA, w_krB=w_krB, w_pin=w_pin, w_gate=w_gate,
        w_uqA=w_uqA.reshape(384, 768), w_uqB=w_uqB.reshape(384, 768), w_uk=w_uk, w_uv=w_uv,
        w_pool=np.asarray(w_pool, f32).reshape(512, 128), w_ba=np.asarray(w_branch_attn, f32)[0],
        w_bp=np.asarray(w_branch_pool, f32)[0], w_out=np.asarray(w_out, f32)[0],
        w_ff1=np.asarray(w_ff1, f32)[0], w_ff2=np.asarray(w_ff2, f32)[0],
        w_pe=np.asarray(w_ple_proj, f32)[0], w_pg=np.asarray(w_ple_gate, f32)[0],
    )
    shared = {k: np.ascontiguousarray(v) for k, v in shared.items()}
    in_maps = []
    xb = x2.reshape(NJ, 8, 128, D)
    pb = p2.reshape(NJ, 8, 128, 256)
    posb = pos.reshape(NJ, 8, 128)
    xpad = np.concatenate([np.zeros((16, D), f32), x2], 0)
    kk = np.arange(128)[:, None]
    qq = np.arange(128)[None, :]
    for c in range(NCORES):
        m = dict(shared)
        m["x_own"] = np.ascontiguousarray(xb[:, c].reshape(NJ * 128, D))
        m["p_own"] = np.ascontiguousarray(pb[:, c].reshape(NJ * 128, 256))
        m["pos_own"] = np.ascontiguousarray(posb[:, c].reshape(1, NJ * 128))
        halo = np.stack([xpad[(8 * j + c) * 128:(8 * j + c) * 128 + 16] for j in range(NJ)], 0)
        m["x_halo"] = np.ascontiguousarray(halo.reshape(NJ * 16, D))
        mk = np.full((128, 8, 128), -30000.0, f32)
        for i in range(8):
            if i < c:
                mk[:, i, :] = 0.0
            elif i == c:
                mk[:, i, :] = np.where(kk <= qq, 0.0, -30000.0)
        m["maskb"] = mk.reshape(128, 1024)
        m["band0"] = (band0 if c == 0 else bandN).reshape(128, 512).copy()
        in_maps.append(m)
    return in_maps


def unpack_output(R, results):
    NJ = R
    out = np.zeros((NJ, 8, 128, D), np.float32)
    for c in range(NCORES):
        out[:, c] = np.asarray(results[c]["out_own"], np.float32).reshape(NJ, 128, D)
    return out.reshape(1, NJ * 1024, D)


_NC_CACHE = {}


def kernel(**inputs):
    R = 16
    if R not in _NC_CACHE:
        _NC_CACHE[R] = build(R)
    nc = _NC_CACHE[R]
    in_maps = pack_inputs(R, **inputs)
    res = run_bass_kernel_spmd(nc, in_maps, core_ids=list(range(NCORES)))
    return unpack_output(R, res.results)
```

```python
import math
import contextlib
import numpy as np
import concourse.bass as bass
import concourse.mybir as mybir
from concourse.bass_utils import run_bass_kernel_spmd

F32 = mybir.dt.float32
BF16 = mybir.dt.bfloat16
I32 = mybir.dt.int32
ALU = mybir.AluOpType
AF = mybir.ActivationFunctionType

NCORES = 8
D = 1024
EPS = 1e-6
ENGS = ("pe", "act", "dve", "pool", "sp")


class Tok:
    __slots__ = ("w", "r")

    def __init__(self):
        self.w = []
        self.r = []


def A(t):
    return ("+", t)


class Op:
    __slots__ = ("eng", "fn", "deps", "is_dma", "signal", "clock", "needs_inc", "idx")

    def __init__(self, eng, fn, deps, is_dma):
        self.eng = eng
        self.fn = fn
        self.deps = deps
        self.is_dma = is_dma
        self.signal = None
        self.clock = None
        self.needs_inc = is_dma
        self.idx = 0


class Prog:
    def __init__(self, nc, n_dma_sems=48):
        self.nc = nc
        self.ops = []
        self.per_eng = {e: [] for e in ENGS}
        self.n_dma_sems = n_dma_sems
        self.dma_rr = 0
        self.dma_last = [None] * n_dma_sems
        self.dma_cnt = [0] * n_dma_sems

    def tok(self):
        return Tok()

    def _mk(self, eng, fn, reads, writes, is_dma, extra=()):
        deps = []
        seen = set()

        def add(o):
            if o is None or id(o) in seen:
                return
            if eng == "pe" and o.eng == "pe" and not o.is_dma:
                return
            seen.add(id(o))
            deps.append(o)

        for t in reads:
            for o in t.w:
                add(o)
        for t in writes:
            if isinstance(t, tuple):
                for o in t[1].r:
                    add(o)
            else:
                for o in t.r:
                    add(o)
                for o in t.w:
                    add(o)
        for o in extra:
            add(o)
        op = Op(eng, fn, deps, is_dma)
        for d in deps:
            d.needs_inc = True
        for t in reads:
            t.r.append(op)
        for t in writes:
            if isinstance(t, tuple):
                t[1].w.append(op)
            else:
                t.w = [op]
                t.r = []
        op.idx = len(self.ops)
        self.ops.append(op)
        self.per_eng[eng].append(op)
        return op

    def op(self, eng, fn, reads=(), writes=(), extra=()):
        return self._mk(eng, fn, reads, writes, False, extra)

    def dma(self, eng, out, in_, reads=(), writes=()):
        s = self.dma_rr % self.n_dma_sems
        self.dma_rr += 1
        prev = self.dma_last[s]
        self.dma_cnt[s] += 1
        val = 16 * self.dma_cnt[s]

        def fn(e, out=out, in_=in_):
            return e.dma_start(out=out, in_=in_)

        op = self._mk(eng, fn, reads, writes, True, extra=(prev,) if prev is not None else ())
        op.signal = (("dma", s), val)
        self.dma_last[s] = op
        return op

    def barrier(self):
        lasts = []
        for e in ENGS:
            for o in reversed(self.per_eng[e]):
                if not o.is_dma:
                    lasts.append(o)
                    break
        lasts += [o for o in self.dma_last if o is not None]
        for e in ENGS:
            self._mk(e, lambda eng: eng.nop(), (), (), False, extra=lasts)

    def emit(self):
        nc = self.nc
        cnt = {e: 0 for e in ENGS}
        for e in ENGS:
            for op in self.per_eng[e]:
                if op.is_dma:
                    continue
                if op.needs_inc:
                    cnt[e] += 1
                    op.signal = (("eng", e), cnt[e])
        known = {e: {} for e in ENGS}
        waits = {}
        for op in self.ops:
            kn = known[op.eng]
            best = {}
            for d in op.deps:
                k, v = d.signal
                if kn.get(k, 0) < v:
                    if best.get(k, 0) < v:
                        best[k] = v
                    for kk, vv in d.clock.items():
                        if kn.get(kk, 0) < vv:
                            kn[kk] = vv
            waits[op.idx] = list(best.items())
            if op.signal is not None:
                c = dict(kn)
                k, v = op.signal
                if c.get(k, 0) < v:
                    c[k] = v
                op.clock = c
        with contextlib.ExitStack() as st:
            sems = {}
            for e in ENGS:
                sems[("eng", e)] = st.enter_context(nc.semaphore(f"s_{e}"))
            for i in range(self.n_dma_sems):
                sems[("dma", i)] = st.enter_context(nc.semaphore(f"s_dma{i}"))
            block = st.enter_context(nc.Block())
            engmap = {"pe": block.tensor, "act": block.scalar, "dve": block.vector,
                      "pool": block.gpsimd, "sp": block.sync}

            def make(e):
                ops = self.per_eng[e]

                def body(eng):
                    for op in ops:
                        for k, v in waits[op.idx]:
                            eng.wait_ge(sems[k], v)
                        ins = op.fn(eng)
                        if op.signal is not None:
                            k, v = op.signal
                            ins.then_inc(sems[k], 16 if op.is_dma else 1)
                return body

            for e in ENGS:
                if self.per_eng[e]:
                    engmap[e](make(e))


class Rot:
    def __init__(self, items):
        self.items = items
        self.i = 0

    def next(self):
        it = self.items[self.i % len(self.items)]
        self.i += 1
        return it


V_GPM, V_GQ, V_GKV, V_BG, V_PSC, V_GPOM, V_GPRM, V_GPOL, V_GPLE, V_INVF, V_N = 0, 8, 11, 13, 29, 33, 41, 49, 57, 65, 66


def build(R, debug=False):
    NJ = R
    T = 128 * NJ
    S = 1024 * R
    NB = 8 * R
    NP = R // 4
    assert R % 4 == 0
    nc = bass.Bass("TRN2", target_bir_lowering=False)

    def din(name, shape, dt=F32):
        return nc.dram_tensor(name, list(shape), dt, kind="ExternalInput").ap()

    def dscr(name, shape, dt):
        return nc.dram_tensor(name, list(shape), dt, kind="ExternalOutput" if debug else "Internal").ap()

    x_all = din("x_all", [S, D])
    x_own = din("x_own", [T, D])
    x_halo = din("x_halo", [16 * NJ, D])
    p_own = din("p_own", [T, 256])
    pos_all = din("pos_all", [1, S], I32)
    pos_own = din("pos_own", [1, T], I32)
    maskb_d = din("maskb", [128, 8 * 128])
    vecs_d = din("vecs", [128, V_N])
    band0_d = din("band0", [128, 4 * 128])
    bandN_d = din("bandN", [128, 4 * 128])
    bandH_d = din("bandH", [16, 4 * 128])
    w_q = din("w_q", [D, 384])
    w_kv = din("w_kv", [D, 256])
    w_krA = din("w_krA", [D, 96])
    w_krB = din("w_krB", [D, 96])
    w_pin = din("w_pin", [D, 512])
    w_gate = din("w_gate", [D, 2048])
    w_uqA = din("w_uqA", [384, 8 * 96])
    w_uqB = din("w_uqB", [384, 8 * 96])
    w_uk = din("w_uk", [256, 512])
    w_uv = din("w_uv", [256, 512])
    w_pool = din("w_pool", [512, 128])
    w_ba = din("w_ba", [512, D])
    w_bp = din("w_bp", [512, D])
    w_out = din("w_out", [D, D])
    w_ff1 = din("w_ff1", [D, 4096])
    w_ff2 = din("w_ff2", [4096, D])
    w_pe = din("w_pe", [256, D])
    w_pg = din("w_pg", [D, D])
    out_own = nc.dram_tensor("out_own", [T, D], F32, kind="ExternalOutput").ap()

    kscr = dscr("kscr", [8, 96, S], BF16)
    vscr = dscr("vscr", [8, 128, NB * 65], BF16)
    hscr = dscr("hscr", [128, 8 * T], F32)
    tab_all = dscr("tab_all", [3, 16, S], F32)
    tab_own = dscr("tab_own", [3, 16, T], F32)
    if debug:
        qt_dbg = nc.dram_tensor("qt_dbg", [96, 8 * T], BF16, kind="ExternalOutput").ap()
        at_dbg = nc.dram_tensor("at_dbg", [64, 8 * T], BF16, kind="ExternalOutput").ap()

    P = Prog(nc)

    def mm(out, lhsT, rhs, start, stop, reads, writes):
        return P.op("pe", lambda e: e.matmul(out, lhsT=lhsT, rhs=rhs, start=start, stop=stop), reads, writes)

    def tr(out, in_, ident, reads, writes):
        return P.op("pe", lambda e: e.transpose(out=out, in_=in_, identity=ident), reads, writes)

    def act(out, in_, func, reads, writes, bias=None, scale=None):
        kw = {}
        if bias is not None:
            kw["bias"] = bias
        if scale is not None:
            kw["scale"] = scale
        return P.op("act", lambda e: e.activation(out=out, in_=in_, func=func, **kw), reads, writes)

    def ts(eng, out, in0, s1, s2, op0, op1, reads, writes):
        if op1 is None:
            return P.op(eng, lambda e: e.tensor_scalar(out=out, in0=in0, scalar1=s1, scalar2=None, op0=op0), reads, writes)
        return P.op(eng, lambda e: e.tensor_scalar(out=out, in0=in0, scalar1=s1, scalar2=s2, op0=op0, op1=op1), reads, writes)

    def tt(eng, out, in0, in1, op, reads, writes):
        return P.op(eng, lambda e: e.tensor_tensor(out=out, in0=in0, in1=in1, op=op), reads, writes)

    def stt(eng, out, in0, scalar, in1, op0, op1, reads, writes, accum_out=None):
        if accum_out is None:
            return P.op(eng, lambda e: e.scalar_tensor_tensor(out=out, in0=in0, scalar=scalar, in1=in1, op0=op0, op1=op1), reads, writes)
        return P.op(eng, lambda e: e.scalar_tensor_tensor(out=out, in0=in0, scalar=scalar, in1=in1, op0=op0, op1=op1,
                                                          accum_out=accum_out), reads, writes)

    def cp(eng, out, in_, reads, writes):
        return P.op(eng, lambda e: e.tensor_copy(out=out, in_=in_), reads, writes)

    def recip(out, in_, reads, writes):
        return P.op("dve", lambda e: e.reciprocal(out=out, in_=in_), reads, writes)

    def memset(eng, ap, val, writes):
        return P.op(eng, lambda e: e.memset(ap, val), (), writes)

    def rstd_from_sum(dst, src, n, reads, tokd):
        ts("dve", dst, src, 1.0 / n, EPS, ALU.mult, ALU.add, reads, [tokd])
        act(dst, dst, AF.Sqrt, [tokd], [tokd])
        recip(dst, dst, [tokd], [tokd])

    with contextlib.ExitStack() as st0:
        def sb(st, name, shape, dt):
            return st.enter_context(nc.sbuf_tensor(name, list(shape), dt))

        pp = [st0.enter_context(nc.psum_tensor(f"pp{i}", [128, 2, 512], F32)) for i in range(4)]
        bank_items = []
        for i in range(4):
            for j in range(2):
                bank_items.append((pp[i][:, j, :], P.tok()))
        banks = Rot(bank_items)

        identf = sb(st0, "identf", [128, 128], F32)
        identb = sb(st0, "identb", [128, 128], BF16)
        ones_bf = sb(st0, "ones_bf", [128, 128], BF16)
        sel = sb(st0, "sel", [65, 64], F32)
        vecs = sb(st0, "vecs_sb", [128, V_N], F32)
        t_c = P.tok()
        P.op("pool", lambda e: e.memset(identf[:], 1.0), (), [t_c])
        P.op("pool", lambda e: e.affine_select(out=identf[:], in_=identf[:], pattern=[[-1, 128]],
                                               compare_op=ALU.is_equal, fill=0.0, base=0, channel_multiplier=1),
             [t_c], [t_c])
        cp("dve", identb[:], identf[:], [t_c], [t_c])
        memset("dve", ones_bf[:], 1.0, [t_c])
        memset("dve", sel[0:64, :], 0.0, [t_c])
        memset("dve", sel[64:65, :], 1.0, [t_c])
        P.dma("sp", vecs[:], vecs_d, (), [t_c])

        def transpose_to(dst3, src_tok_ap, npart, ident, t_src, t_dst, evac_eng="act", f32=False):
            if f32:
                for half in range(2):
                    bk, tb = banks.next()
                    v = bk.rearrange("p (c t) -> p c t", c=4)
                    for c in range(4):
                        cc = half * 4 + c
                        tr(v[:, c, 0:npart], src_tok_ap[:, cc * 128:(cc + 1) * 128], ident, [t_src, t_c], [tb])
                    wd = t_dst if half == 0 else A(t_dst[1] if isinstance(t_dst, tuple) else t_dst)
                    if evac_eng == "act":
                        act(dst3[:, half * 4:half * 4 + 4, :], v[:, :, 0:npart], AF.Copy, [tb], [wd])
                    else:
                        cp(evac_eng, dst3[:, half * 4:half * 4 + 4, :], v[:, :, 0:npart], [tb], [wd])
            else:
                bk, tb = banks.next()
                v = bk.bitcast(BF16).rearrange("p (c t) -> p c t", c=8)
                for c in range(8):
                    tr(v[:, c, 0:npart], src_tok_ap[:, c * 128:(c + 1) * 128], ident, [t_src, t_c], [tb])
                if evac_eng == "act":
                    act(dst3, v[:, :, 0:npart], AF.Copy, [tb], [t_dst])
                else:
                    cp(evac_eng, dst3, v[:, :, 0:npart], [tb], [t_dst])

        def norm_front(st, tag, xg, t_xg, nb, npart, junk, xs_rot):
            ssq = sb(st, f"ssq_{tag}", [128, nb], F32)
            t_ss = P.tok()
            for b in range(nb):
                jk, t_jk = junk.next()
                stt("dve", jk[0:npart, :], xg[0:npart, b, :], 1.0, xg[0:npart, b, :], ALU.mult, ALU.mult,
                    [t_xg], [t_jk, t_ss if b == 0 else A(t_ss)], accum_out=ssq[0:npart, b:b + 1])
            rstd_from_sum(ssq[0:npart, :], ssq[0:npart, :], D, [t_ss], t_ss)
            outl = []
            for b in range(nb):
                xs, t_xs = xs_rot.next()
                ts("dve", xs[0:npart, :], xg[0:npart, b, :], ssq[0:npart, b:b + 1], None, ALU.mult, None,
                   [t_xg, t_ss], [t_xs])
                outl.append((xs, t_xs))
            return outl

        def norm_back(xsl, npart, dst_fn, t_dst):
            for b, (xs, t_xs) in enumerate(xsl):
                transpose_to(dst_fn(b), xs[0:npart, :], npart, identb[0:npart, 0:npart], t_xs, t_dst if b == 0 else A(t_dst))

        def norm_blocks(st, tag, xg, t_xg, nb, npart, dst_fn, t_dst, junk, t_junk, xs_rot):
            norm_back(norm_front(st, tag, xg, t_xg, nb, npart, junk, xs_rot), npart, dst_fn, t_dst)

        def load_w(st, dst3, src, nch, ncols, gain_col0=None, prow=128, stage_rot=None, t_w=None, queue="pool"):
            if gain_col0 is None:
                for c in range(nch):
                    for c0 in range(0, ncols, 2048):
                        c1 = min(ncols, c0 + 2048)
                        stg, t_s = stage_rot.next()
                        P.dma("sp", stg[0:prow, 0:c1 - c0], src[c * prow:(c + 1) * prow, c0:c1], (), [t_s])
                        if (c % 2) == 0:
                            cp("dve", dst3[:, c, c0:c1], stg[0:prow, 0:c1 - c0], [t_s], [A(t_w)])
                        else:
                            act(dst3[:, c, c0:c1], stg[0:prow, 0:c1 - c0], AF.Copy, [t_s], [A(t_w)])
            else:
                for c in range(nch):
                    for c0 in range(0, ncols, 2048):
                        c1 = min(ncols, c0 + 2048)
                        stg, t_s = stage_rot.next()
                        P.dma("sp", stg[0:prow, 0:c1 - c0], src[c * prow:(c + 1) * prow, c0:c1], (), [t_s])
                        act(dst3[:, c, c0:c1], stg[0:prow, 0:c1 - c0], AF.Copy, [t_s, t_c], [A(t_w)],
                            scale=vecs[0:prow, gain_col0 + c:gain_col0 + c + 1])

        def rope_tables(st, tag, pos_d, ntok, dst):
            L = ntok // 8
            posi = sb(st, f"posi_{tag}", [128, L], I32)
            ang = sb(st, f"ang_{tag}", [128, L], F32)
            kf = sb(st, f"kf_{tag}", [128, L], F32)
            ki = sb(st, f"ki_{tag}", [128, L], I32)
            r = sb(st, f"r_{tag}", [128, L], F32)
            y = sb(st, f"y_{tag}", [128, L], F32)
            m = sb(st, f"m_{tag}", [128, L], F32)
            t_p, t_a, t_k, t_r, t_y, t_m = (P.tok() for _ in range(6))
            for g in range(8):
                P.dma("sp", posi[16 * g:16 * g + 16, :], pos_d[0:1, g * L:(g + 1) * L].broadcast_to([16, L]), (), [A(t_p)])
            cp("dve", ang[:], posi[:], [t_p], [t_a])
            ts("dve", ang[:], ang[:], vecs[:, V_INVF:V_INVF + 1], None, ALU.mult, None, [t_a, t_c], [t_a])
            ts("dve", kf[:], ang[:], 1.0 / (2 * math.pi), None, ALU.mult, None, [t_a], [t_k])
            cp("dve", ki[:], kf[:], [t_k], [t_k])
            cp("dve", kf[:], ki[:], [t_k], [t_k])
            stt("dve", r[:], kf[:], -2 * math.pi, ang[:], ALU.mult, ALU.add, [t_k, t_a], [t_r])
            ts("dve", y[:], r[:], 0.5 * math.pi, None, ALU.add, None, [t_r], [t_y])
            ts("dve", m[:], y[:], math.pi, -2 * math.pi, ALU.is_gt, ALU.mult, [t_y], [t_m])
            tt("dve", y[:], y[:], m[:], ALU.add, [t_y, t_m], [t_y])
            ts("dve", y[:], y[:], 3.141592, -3.141592, ALU.min, ALU.max, [t_y], [t_y])
            act(y[:], y[:], AF.Sin, [t_y], [t_y])
            ts("dve", r[:], r[:], 3.141592, -3.141592, ALU.min, ALU.max, [t_r], [t_r])
            act(r[:], r[:], AF.Sin, [t_r], [t_r])
            ts("dve", m[:], r[:], -1.0, None, ALU.mult, None, [t_r, t_m], [t_m])
            t_out = P.tok()
            for g in range(8):
                P.dma("sp", dst[0, :, g * L:(g + 1) * L], y[16 * g:16 * g + 16, :], [t_y], [A(t_out)])
                P.dma("sp", dst[1, :, g * L:(g + 1) * L], r[16 * g:16 * g + 16, :], [t_r], [A(t_out)])
                P.dma("sp", dst[2, :, g * L:(g + 1) * L], m[16 * g:16 * g + 16, :], [t_m], [A(t_out)])
            return t_out

        def load_tab(cosd, sind, tab, t_tab, c0, c1, t_dst):
            n = c1 - c0
            P.dma("sp", cosd[64:80, 0:n], tab[0, :, c0:c1], [t_tab], [t_dst])
            P.dma("sp", cosd[80:96, 0:n], tab[0, :, c0:c1], [t_tab], [A(t_dst)])
            P.dma("sp", sind[64:80, 0:n], tab[2, :, c0:c1], [t_tab], [A(t_dst)])
            P.dma("sp", sind[80:96, 0:n], tab[1, :, c0:c1], [t_tab], [A(t_dst)])

        with contextlib.ExitStack() as stA:
            attnT = sb(stA, "attnT", [64, 8, T], BF16)
            t_attn = [P.tok() for _ in range(NP * 8)]
            with contextlib.ExitStack() as stQ:
                QT = sb(stQ, "QT", [96, 8, T], BF16)
                t_QT = P.tok()

                with contextlib.ExitStack() as st1:
                    Wq = sb(st1, "Wq", [128, 8, 384], BF16)
                    Wkv = sb(st1, "Wkv", [128, 8, 256], BF16)
                    WkrA = sb(st1, "WkrA", [128, 8, 96], BF16)
                    WkrB = sb(st1, "WkrB", [128, 8, 96], BF16)
                    WuqA = sb(st1, "WuqA", [128, 3, 768], BF16)
                    WuqB = sb(st1, "WuqB", [128, 3, 768], BF16)
                    Wuk = sb(st1, "Wuk", [128, 2, 512], BF16)
                    Wuv = sb(st1, "Wuv", [128, 2, 512], BF16)
                    t_w1 = P.tok()
                    with contextlib.ExitStack() as stS:
                        stage = Rot([(sb(stS, f"stage{i}", [128, 2048], F32), P.tok()) for i in range(4)])
                        t_tab_all = rope_tables(stS, "a", pos_all, S, tab_all)
                        t_tab_own = rope_tables(stS, "o", pos_own, T, tab_own)
                        load_w(stS, Wq, w_q, 8, 384, V_GPM, stage_rot=stage, t_w=t_w1)
                        load_w(stS, Wkv, w_kv, 8, 256, V_GPM, stage_rot=stage, t_w=t_w1)
                        load_w(stS, WkrA, w_krA, 8, 96, V_GPM, stage_rot=stage, t_w=t_w1)
                        load_w(stS, WkrB, w_krB, 8, 96, V_GPM, stage_rot=stage, t_w=t_w1)
                        load_w(stS, WuqA, w_uqA, 3, 768, V_GQ, stage_rot=stage, t_w=t_w1)
                        load_w(stS, WuqB, w_uqB, 3, 768, V_GQ, stage_rot=stage, t_w=t_w1)
                        load_w(stS, Wuk, w_uk, 2, 512, V_GKV, stage_rot=stage, t_w=t_w1)
                        load_w(stS, Wuv, w_uv, 2, 512, V_GKV, stage_rot=stage, t_w=t_w1)
                        P.barrier()

                    junk = Rot([(sb(st1, f"junk{i}", [128, 1024], BF16), P.tok()) for i in range(2)])
                    t_junk = None
                    xs_rot = Rot([(sb(st1, f"xs{i}", [128, 1024], BF16), P.tok()) for i in range(4)])
                    cosd = Rot([(sb(st1, f"cosd{i}", [96, 512], F32), sb(st1, f"sind{i}", [96, 512], F32), P.tok())
                                for i in range(2)])
                    t1r = Rot([(sb(st1, f"t1r{i}", [96, 512], F32), P.tok()) for i in range(2)])
                    t2r = Rot([(sb(st1, f"t2r{i}", [96, 512], F32), P.tok()) for i in range(2)])
                    rsb = Rot([(sb(st1, f"rsb{i}", [128, 512], F32), P.tok()) for i in range(2)])
                    sqr = Rot([(sb(st1, f"sqr{i}", [128, 3, 512], BF16), P.tok()) for i in range(2)])
                    xg_rot = Rot([(sb(st1, f"xg_{i}", [128, 4, 1024], F32), P.tok()) for i in range(2)])
                    aT_rot = Rot([(sb(st1, f"aT_{i}", [128, 8, 512], BF16), P.tok()) for i in range(2)])

                    with contextlib.ExitStack() as st:
                        qd_rot = Rot([(sb(st, f"qd{i}", [128, 3, 512], BF16), P.tok()) for i in range(2)])
                        for ti in range(NJ // 4):
                            c0 = ti * 512
                            xg, t_xg = xg_rot.next()
                            P.dma("sp", xg[:], x_own[c0:c0 + 512, :].rearrange("(b p) d -> p b d", p=128), (), [t_xg])
                            aT, t_aT = aT_rot.next()
                            norm_blocks(st, f"p0_{ti}", xg, t_xg, 4, 128, lambda b: aT[:, :, b * 128:(b + 1) * 128],
                                        t_aT, junk, t_junk, xs_rot)
                            qd, t_qd = qd_rot.next()
                            sq, t_sq = sqr.next()
                            for c in range(3):
                                bk, tb = banks.next()
                                for d in range(8):
                                    mm(bk, Wq[:, d, c * 128:(c + 1) * 128], aT[:, d, :], d == 0, d == 7, [t_w1, t_aT], [tb])
                                act(qd[:, c, :], bk, AF.Copy, [tb], [t_qd if c == 0 else A(t_qd)])
                                act(sq[:, c, :], bk, AF.Square, [tb], [t_sq if c == 0 else A(t_sq)])
                            bk, tb = banks.next()
                            for c in range(3):
                                mm(bk, ones_bf[:], sq[:, c, :], c == 0, c == 2, [t_c, t_sq], [tb])
                            rs, t_rs = rsb.next()
                            rstd_from_sum(rs[:], bk, 384, [tb], t_rs)
                            cs, sn, t_cs = cosd.next()
                            load_tab(cs, sn, tab_own, t_tab_own, c0, c0 + 512, t_cs)
                            tt("dve", cs[64:96, :], cs[64:96, :], rs[64:96, :], ALU.mult, [t_cs, t_rs], [t_cs])
                            tt("dve", sn[64:96, :], sn[64:96, :], rs[64:96, :], ALU.mult, [t_cs, t_rs], [t_cs])
                            for h in range(8):
                                bkA, tbA = banks.next()
                                bkB, tbB = banks.next()
                                for c in range(3):
                                    mm(bkA[0:96, :], WuqA[:, c, h * 96:(h + 1) * 96], qd[:, c, :], c == 0, c == 2,
                                       [t_w1, t_qd], [tbA])
                                for c in range(3):
                                    mm(bkB[0:96, :], WuqB[:, c, h * 96:(h + 1) * 96], qd[:, c, :], c == 0, c == 2,
                                       [t_w1, t_qd], [tbB])
                                tt("dve", QT[0:64, h, c0:c0 + 512], bkA[0:64, :], rs[0:64, :], ALU.mult, [tbA, t_rs], [A(t_QT)])
                                t1, t_t1 = t1r.next()
                                t2, t_t2 = t2r.next()
                                tt("dve", t1[64:96, :], bkA[64:96, :], cs[64:96, :], ALU.mult, [tbA, t_cs], [t_t1])
                                tt("dve", t2[64:96, :], bkB[64:96, :], sn[64:96, :], ALU.mult, [tbB, t_cs], [t_t2])
                                tt("pool", QT[64:96, h, c0:c0 + 512], t1[64:96, :], t2[64:96, :], ALU.add,
                                   [t_t1, t_t2], [A(t_QT)])
                        if debug:
                            P.dma("sp", qt_dbg, QT[:].rearrange("p h t -> p (h t)"), [t_QT], [])
                        P.barrier()

                    with contextlib.ExitStack() as st:
                        kvd_rot = Rot([(sb(st, f"kvd{i}", [128, 2, 512], BF16), P.tok()) for i in range(2)])
                        KTn_rot = Rot([(sb(st, f"KTn{i}", [128, 4, 512], BF16), P.tok()) for i in range(2)])
                        krT_rot = Rot([(sb(st, f"krT{i}", [96, 512], BF16), P.tok()) for i in range(2)])
                        Va_rot = Rot([(sb(st, f"Va{i}", [128, 8, 4, 65], BF16), P.tok()) for i in range(2)])
                        rcol_rot = Rot([(sb(st, f"rcol{i}", [128, 4], F32), P.tok()) for i in range(2)])
                        for (va, t_va) in Va_rot.items:
                            memset("pool", va[:].rearrange("p a b e -> p (a b e)"), 1.0, [t_va])
                        t_kscr = P.tok()
                        t_vscr = P.tok()
                        if debug:
                            print('sbuf remaining pass1', nc.sbuf_bytes_remaining)
                        def stageA1(hu):
                            tok0 = hu * 512
                            xg, t_xg = xg_rot.next()
                            P.dma("sp", xg[:], x_all[tok0:tok0 + 512, :].rearrange("(b p) d -> p b d", p=128), (), [t_xg])
                            return norm_front(st, f"p1_{hu}", xg, t_xg, 4, 128, junk, xs_rot)

                        def stageA2(xsl):
                            aT, t_aT = aT_rot.next()
                            norm_back(xsl, 128, lambda b: aT[:, :, b * 128:(b + 1) * 128], t_aT)
                            return aT, t_aT

                        def stageB1(hu, aT, t_aT):
                            tok0 = hu * 512
                            KTn, t_KTn = KTn_rot.next()
                            krT, t_krT = krT_rot.next()
                            Va, t_Va = Va_rot.next()
                            kvd, t_kvd = kvd_rot.next()
                            sq, t_sq = sqr.next()
                            for c in range(2):
                                bk, tb = banks.next()
                                for d in range(8):
                                    mm(bk, Wkv[:, d, c * 128:(c + 1) * 128], aT[:, d, :], d == 0, d == 7, [t_w1, t_aT], [tb])
                                act(kvd[:, c, :], bk, AF.Copy, [tb], [t_kvd if c == 0 else A(t_kvd)])
                                act(sq[:, c, :], bk, AF.Square, [tb], [t_sq if c == 0 else A(t_sq)])
                            bkA, tbA = banks.next()
                            bkB, tbB = banks.next()
                            for d in range(8):
                                mm(bkA[0:96, :], WkrA[:, d, :], aT[:, d, :], d == 0, d == 7, [t_w1, t_aT], [tbA])
                            for d in range(8):
                                mm(bkB[0:96, :], WkrB[:, d, :], aT[:, d, :], d == 0, d == 7, [t_w1, t_aT], [tbB])
                            cs, sn, t_cs = cosd.next()
                            load_tab(cs, sn, tab_all, t_tab_all, tok0, tok0 + 512, t_cs)
                            t1, t_t1 = t1r.next()
                            t2, t_t2 = t2r.next()
                            tt("dve", t1[64:96, :], bkA[64:96, :], cs[64:96, :], ALU.mult, [tbA, t_cs], [t_t1])
                            tt("dve", t2[64:96, :], bkB[64:96, :], sn[64:96, :], ALU.mult, [tbB, t_cs], [t_t2])
                            tt("pool", krT[64:96, :], t1[64:96, :], t2[64:96, :], ALU.add, [t_t1, t_t2], [t_krT])
                            bk, tb = banks.next()
                            for c in range(2):
                                mm(bk, ones_bf[:], sq[:, c, :], c == 0, c == 1, [t_c, t_sq], [tb])
                            return dict(tok0=tok0, hu=hu, KTn=KTn, t_KTn=t_KTn, krT=krT, t_krT=t_krT, Va=Va, t_Va=t_Va, kvd=kvd,
                                        t_kvd=t_kvd, sq=sq, t_sq=t_sq, bk=bk, tb=tb)

                        def stageB2(S_):
                            tok0, hu, KTn, t_KTn, krT, t_krT, Va, t_Va = (S_[k] for k in ("tok0", "hu", "KTn", "t_KTn", "krT", "t_krT", "Va", "t_Va"))
                            kvd, t_kvd, sq, t_sq, bk, tb = (S_[k] for k in ("kvd", "t_kvd", "sq", "t_sq", "bk", "tb"))
                            rs, t_rs = rsb.next()
                            rstd_from_sum(rs[:], bk, 256, [tb], t_rs)
                            bk2, tb2 = banks.next()
                            for b in range(4):
                                for c in range(2):
                                    mm(bk2[:, b:b + 1], sq[:, c, b * 128:(b + 1) * 128], ones_bf[:, 0:1], c == 0, c == 1,
                                       [t_c, t_sq], [tb2])
                            rc, t_rc = rcol_rot.next()
                            rstd_from_sum(rc[:], bk2[:, 0:4], 256, [tb2], t_rc)
                            for hp in range(4):
                                bk, tb = banks.next()
                                for c in range(2):
                                    mm(bk, Wuk[:, c, hp * 128:(hp + 1) * 128], kvd[:, c, :], c == 0, c == 1, [t_w1, t_kvd], [tb])
                                tt("dve", KTn[:, hp, :], bk, rs[:], ALU.mult, [tb, t_rs], [t_KTn if hp == 0 else A(t_KTn)])
                            for b in range(4):
                                bk, tb = banks.next()
                                for c in range(2):
                                    mm(bk, kvd[:, c, b * 128:(b + 1) * 128], Wuv[:, c, :], c == 0, c == 1, [t_w1, t_kvd], [tb])
                                act(Va[:, :, b, 0:64], bk.rearrange("p (h d) -> p h d", h=8), AF.Copy,
                                    [tb, t_rc], [t_Va if b == 0 else A(t_Va)], scale=rc[:, b:b + 1])
                            for h in range(8):
                                P.dma("sp", kscr[h, 0:64, tok0:tok0 + 512],
                                      KTn[(h % 2) * 64:(h % 2) * 64 + 64, h // 2, :], [t_KTn], [A(t_kscr)])
                                P.dma("sp", kscr[h, 64:96, tok0:tok0 + 512], krT[64:96, :], [t_krT], [A(t_kscr)])
                                P.dma("sp", vscr[h, :, hu * 4 * 65:(hu + 1) * 4 * 65],
                                      Va[:, h, :, :].rearrange("p b e -> p (b e)"), [t_Va], [A(t_vscr)])

                        cur = stageA2(stageA1(0))
                        for hu in range(2 * R):
                            xsl = stageA1(hu + 1) if hu + 1 < 2 * R else None
                            S_ = stageB1(hu, *cur)
                            if xsl is not None:
                                cur = stageA2(xsl)
                            stageB2(S_)
                        P.barrier()

                with contextlib.ExitStack() as st:
                    KT_rot = Rot([(sb(st, f"KT{i}", [96, S], BF16), sb(st, f"VH{i}", [128, NB, 65], BF16), P.tok())
                                  for i in range(2)])
                    maskb = sb(st, "maskb_sb", [128, 8, 128], BF16)
                    t_mask = P.tok()
                    mstg = sb(st, "mask_stg", [128, 1024], F32)
                    P.dma("sp", mstg[:], maskb_d, (), [t_mask])
                    cp("dve", maskb[:].rearrange("p i q -> p (i q)"), mstg[:], [t_mask], [t_mask])
                    PT_rot = Rot([(sb(st, f"PT{i}", [128, 2, 512], BF16), P.tok()) for i in range(3)])
                    osb_rot = Rot([(sb(st, f"osb{i}", [65, 512], F32), P.tok()) for i in range(2)])
                    rec_rot = Rot([(sb(st, f"rec{i}", [64, 512], F32), P.tok()) for i in range(2)])
                    S_rot = Rot([(pp[0], P.tok()), (pp[1], P.tok())])
                    O_rot = Rot([(pp[2][:, 0, :], P.tok()), (pp[2][:, 1, :], P.tok())])
                    bc_bank, t_bc = pp[3][:, 0, :], P.tok()
                    scale = 96.0 ** -0.5
                    if debug:
                        print('sbuf remaining pass2', nc.sbuf_bytes_remaining)
                    pending = None

                    def flush(pend):
                        PT, t_PT, O, t_O, VH, t_KV, kbs, W, qo, nkb, fin = pend
                        for u, kb in enumerate(kbs):
                            mm(O[0:65, qo:512], VH[:, kb, :], PT[:, u, 0:W], kb == 0, kb == nkb - 1, [t_KV, t_PT], [t_O])
                        if fin is not None:
                            fin()

                    for h in range(8):
                        KT, VH, t_KV = KT_rot.next()
                        P.dma("sp", KT[:], kscr[h], [t_kscr], [t_KV])
                        P.dma("sp", VH[:].rearrange("p b e -> p (b e)"), vscr[h], [t_vscr], [A(t_KV)])
                        for p in range(NP):
                            O, t_O = O_rot.next()
                            nkb = 32 * p + 32
                            q0 = p * 512

                            def make_fin(h=h, p=p, O=O, t_O=t_O, q0=q0):
                                def fin():
                                    osb, t_osb = osb_rot.next()
                                    cp("dve", osb[:], O[0:65, :], [t_O], [t_osb])
                                    mm(bc_bank[0:64, :], sel[:], osb[:], True, True, [t_c, t_osb], [t_bc])
                                    rec, t_rec = rec_rot.next()
                                    recip(rec[:], bc_bank[0:64, :], [t_bc], [t_rec])
                                    tt("dve", attnT[:, h, q0:q0 + 512], osb[0:64, :], rec[:], ALU.mult, [t_osb, t_rec],
                                       [t_attn[p * 8 + h]])
                                return fin

                            for kb in range(0, nkb, 2):
                                if kb < 32 * p:
                                    W, diag = 512, False
                                else:
                                    jj = (kb - 32 * p) // 8
                                    W, diag = (4 - jj) * 128, True
                                qo = 512 - W
                                Sx, t_S = S_rot.next()
                                for u in range(2):
                                    kbu = kb + u
                                    mm(Sx[:, u, 0:W], KT[:, kbu * 128:(kbu + 1) * 128], QT[:, h, q0 + qo:q0 + 512], True, not diag,
                                       [t_KV, t_QT], [t_S])
                                    if diag:
                                        mm(Sx[:, u, 0:128], identb[:], maskb[:, kbu % 8, :], False, True, [t_c, t_mask], [t_S])
                                if pending is not None:
                                    flush(pending)
                                PT, t_PT = PT_rot.next()
                                act(PT[:, :, 0:W], Sx[:, :, 0:W], AF.Exp, [t_S], [t_PT], scale=scale)
                                pending = (PT, t_PT, O, t_O, VH, t_KV, (kb, kb + 1), W, qo, nkb,
                                           make_fin() if kb + 2 >= nkb else None)
                    flush(pending)
                    if debug:
                        P.dma("sp", at_dbg, attnT[:].rearrange("p h t -> p (h t)"), t_attn, [])
                    P.barrier()
            with contextlib.ExitStack() as st:
                Wpin = sb(st, "Wpin", [128, 8, 512], BF16)
                Wg = sb(st, "Wg", [128, 8, 2048], BF16)
                Wba = sb(st, "Wba", [64, 8, 1024], BF16)
                Wbp = sb(st, "Wbp", [128, 4, 1024], BF16)
                Wout = sb(st, "Wout", [128, 8, 1024], BF16)
                Wpl = sb(st, "Wpl", [128, 4, 128], BF16)
                band0 = sb(st, "band0_sb", [128, 4, 128], BF16)
                bandN = sb(st, "bandN_sb", [128, 4, 128], BF16)
                bandH = sb(st, "bandH_sb", [16, 4, 128], BF16)
                t_w2 = P.tok()
                with contextlib.ExitStack() as stS:
                    stage = Rot([(sb(stS, f"stageb{i}", [128, 2048], F32), P.tok()) for i in range(6)])
                    load_w(stS, Wpin, w_pin, 8, 512, V_GPM, stage_rot=stage, t_w=t_w2)
                    load_w(stS, Wg, w_gate, 8, 2048, V_GPM, stage_rot=stage, t_w=t_w2)
                    load_w(stS, Wba, w_ba, 8, 1024, prow=64, stage_rot=stage, t_w=t_w2)
                    load_w(stS, Wbp, w_bp, 4, 1024, stage_rot=stage, t_w=t_w2)
                    load_w(stS, Wout, w_out, 8, 1024, stage_rot=stage, t_w=t_w2)
                    load_w(stS, Wpl, w_pool, 4, 128, stage_rot=stage, t_w=t_w2)
                    for (bt, bd, npb) in ((band0, band0_d, 128), (bandN, bandN_d, 128), (bandH, bandH_d, 16)):
                        stg, t_s = stage.next()
                        P.dma("sp", stg[0:npb, 0:512], bd, (), [t_s])
                        cp("dve", bt[:].rearrange("p g t -> p (g t)"), stg[0:npb, 0:512], [t_s], [A(t_w2)])
                    P.barrier()
                junk = Rot([(sb(st, f"junkb{i}", [128, 1024], BF16), P.tok()) for i in range(2)])
                t_junk = None
                xs_rot = Rot([(sb(st, f"xsb{i}", [128, 1024], BF16), P.tok()) for i in range(3)])
                xg_rot = Rot([(sb(st, f"xg2_{i}", [128, 2, 1024], F32), P.tok()) for i in range(2)])
                xh_rot = Rot([(sb(st, f"xh2_{i}", [32, 1, 1024], F32), P.tok()) for i in range(1)])
                aT_rot = Rot([(sb(st, f"aT2_{i}", [128, 8, 256], BF16), P.tok()) for i in range(1)])
                aTh_rot = Rot([(sb(st, f"aTh_{i}", [128, 8, 32], BF16), P.tok()) for i in range(1)])
                xT_rot = Rot([(sb(st, f"xT_{i}", [128, 8, 256], F32), P.tok()) for i in range(1)])
                u_rot = Rot([(sb(st, f"u_{i}", [128, 512], BF16), P.tok()) for i in range(2)])
                uh_rot = Rot([(sb(st, f"uh_{i}", [16, 512], BF16), P.tok()) for i in range(2)])
                dT_rot = Rot([(sb(st, f"dT_{i}", [128, 4, 256], BF16), P.tok()) for i in range(1)])
                pl_rot = Rot([(sb(st, f"pl_{i}", [128, 4, 256], BF16), P.tok()) for i in range(1)])
                gate_rot = Rot([(sb(st, f"gate_{i}", [128, 16, 256], BF16), P.tok()) for i in range(1)])
                mg_rot = Rot([(sb(st, f"mg_{i}", [128, 8, 256], BF16), P.tok()) for i in range(1)])
                tmp_rot = Rot([(sb(st, f"tmpa_{i}", [128, 256], F32), P.tok()) for i in range(4)])
                y_rot = Rot([(sb(st, f"y_{i}", [128, 8, 256], F32), P.tok()) for i in range(1)])
                sq_rot = Rot([(sb(st, f"sq8_{i}", [128, 8, 256], BF16), P.tok()) for i in range(1)])
                rs_rot = Rot([(sb(st, f"rs2_{i}", [128, 256], F32), P.tok()) for i in range(2)])
                h1_rot = Rot([(sb(st, f"h1_{i}", [128, 8, 256], F32), P.tok()) for i in range(1)])
                if debug:
                    print('sbuf remaining sweep1', nc.sbuf_bytes_remaining)
                t_hscr = P.tok()
                def s1A1(ti):
                    c0 = ti * 256
                    xg, t_xg = xg_rot.next()
                    P.dma("sp", xg[:], x_own[c0:c0 + 256, :].rearrange("(b p) d -> p b d", p=128), (), [t_xg])
                    xh, t_xh = xh_rot.next()
                    P.dma("sp", xh[:, 0, :], x_halo[ti * 32:(ti + 1) * 32, :], (), [t_xh])
                    xsl = norm_front(st, f"s1_{ti}", xg, t_xg, 2, 128, junk, xs_rot)
                    xsh = norm_front(st, f"s1h_{ti}", xh, t_xh, 1, 32, junk, xs_rot)
                    return dict(ti=ti, c0=c0, xg=xg, t_xg=t_xg, xsl=xsl, xsh=xsh)

                def s1A2(S_):
                    aT, t_aT = aT_rot.next()
                    norm_back(S_["xsl"], 128, lambda b: aT[:, :, b * 128:(b + 1) * 128], t_aT)
                    aTh, t_aTh = aTh_rot.next()
                    norm_back(S_["xsh"], 32, lambda b: aTh[:, :, :], t_aTh)
                    xT, t_xT = xT_rot.next()
                    xg, t_xg = S_["xg"], S_["t_xg"]
                    for b in range(2):
                        transpose_to(xT[:, :, b * 128:(b + 1) * 128], xg[:, b, :], 128, identf[:], t_xg,
                                     t_xT if b == 0 else A(t_xT), f32=True)
                    S_.update(aT=aT, t_aT=t_aT, aTh=aTh, t_aTh=t_aTh, xT=xT, t_xT=t_xT)

                def s1B(S_):
                    ti, c0, aT, t_aT, aTh, t_aTh = (S_[k] for k in ("ti", "c0", "aT", "t_aT", "aTh", "t_aTh"))
                    dT, t_dT = dT_rot.next()
                    for b in range(2):
                        bk, tb = banks.next()
                        for d in range(8):
                            mm(bk, aT[:, d, b * 128:(b + 1) * 128], Wpin[:, d, :], d == 0, d == 7, [t_aT, t_w2], [tb])
                        u, t_u = u_rot.next()
                        act(u[:], bk, AF.Copy, [tb], [t_u])
                        bk, tb = banks.next()
                        for d in range(8):
                            mm(bk[0:16, :], aTh[:, d, b * 16:(b + 1) * 16], Wpin[:, d, :], d == 0, d == 7, [t_aTh, t_w2], [tb])
                        uh, t_uh = uh_rot.next()
                        act(uh[:], bk[0:16, :], AF.Copy, [tb], [t_uh])
                        band = band0 if (ti == 0 and b == 0) else bandN
                        bk, tb = banks.next()
                        for g in range(4):
                            mm(bk[:, g * 128:(g + 1) * 128], u[:, g * 128:(g + 1) * 128], band[:, g, :], True, False,
                               [t_u, t_w2], [tb])
                            mm(bk[:, g * 128:(g + 1) * 128], uh[:, g * 128:(g + 1) * 128], bandH[:, g, :], False, True,
                               [t_uh, t_w2], [tb])
                        act(dT[:, :, b * 128:(b + 1) * 128], bk.rearrange("p (g t) -> p g t", g=4), AF.Copy, [tb],
                            [t_dT if b == 0 else A(t_dT)])
                    pl, t_pl = pl_rot.next()
                    for g in range(4):
                        bk, tb = banks.next()
                        mm(bk[:, 0:256], Wpl[:, g, :], dT[:, g, :], True, True, [t_w2, t_dT], [tb])
                        act(pl[:, g, :], bk[:, 0:256], AF.Copy, [tb, t_c], [t_pl if g == 0 else A(t_pl)],
                            scale=vecs[:, V_PSC + g:V_PSC + g + 1])
                    gate, t_gate = gate_rot.next()
                    for oc in range(16):
                        bk, tb = banks.next()
                        for d in range(8):
                            mm(bk[:, 0:256], Wg[:, d, oc * 128:(oc + 1) * 128], aT[:, d, :], d == 0, d == 7, [t_w2, t_aT], [tb])
                        act(gate[:, oc, :], bk[:, 0:256], AF.Sigmoid, [tb, t_c], [t_gate if oc == 0 else A(t_gate)],
                            bias=vecs[:, V_BG + oc:V_BG + oc + 1])
                    mg, t_mg = mg_rot.next()
                    t_at_tile = [t_attn[(c0 // 512) * 8 + h] for h in range(8)]
                    for oc in range(8):
                        bkA, tbA = banks.next()
                        for h in range(8):
                            mm(bkA[:, 0:256], Wba[:, h, oc * 128:(oc + 1) * 128], attnT[:, h, c0:c0 + 256], h == 0, h == 7,
                               [t_w2, t_at_tile[h]], [tbA])
                        bkB, tbB = banks.next()
                        for g in range(4):
                            mm(bkB[:, 0:256], Wbp[:, g, oc * 128:(oc + 1) * 128], pl[:, g, :], g == 0, g == 3, [t_w2, t_pl], [tbB])
                        ta, t_ta = tmp_rot.next()
                        tt("dve", ta[:], bkA[:, 0:256], gate[:, oc, :], ALU.mult, [tbA, t_gate], [t_ta])
                        tb_, t_tb = tmp_rot.next()
                        tt("dve", tb_[:], bkB[:, 0:256], gate[:, 8 + oc, :], ALU.mult, [tbB, t_gate], [t_tb])
                        tt("pool", mg[:, oc, :], ta[:], tb_[:], ALU.add, [t_ta, t_tb], [t_mg if oc == 0 else A(t_mg)])
                    S_.update(mg=mg, t_mg=t_mg)

                def s1C(S_):
                    c0, mg, t_mg, xT, t_xT = (S_[k] for k in ("c0", "mg", "t_mg", "xT", "t_xT"))
                    y, t_y = y_rot.next()
                    sq, t_sq = sq_rot.next()
                    for oc in range(8):
                        bk, tb = banks.next()
                        for kc in range(8):
                            mm(bk[:, 0:256], Wout[:, kc, oc * 128:(oc + 1) * 128], mg[:, kc, :], kc == 0, kc == 7, [t_w2, t_mg], [tb])
                        act(y[:, oc, :], bk[:, 0:256], AF.Copy, [tb], [t_y if oc == 0 else A(t_y)])
                        act(sq[:, oc, :], bk[:, 0:256], AF.Square, [tb], [t_sq if oc == 0 else A(t_sq)])
                    bk, tb = banks.next()
                    for oc in range(8):
                        mm(bk[:, 0:256], ones_bf[:], sq[:, oc, :], oc == 0, oc == 7, [t_c, t_sq], [tb])
                    rs, t_rs = rs_rot.next()
                    rstd_from_sum(rs[:], bk[:, 0:256], D, [tb], t_rs)
                    h1, t_h1 = h1_rot.next()
                    for oc in range(8):
                        stt("dve", y[:, oc, :], y[:, oc, :], vecs[:, V_GPOM + oc:V_GPOM + oc + 1], rs[:], ALU.mult, ALU.mult,
                            [t_y, t_rs, t_c], [A(t_y)])
                        tt("pool", h1[:, oc, :], y[:, oc, :], xT[:, oc, :], ALU.add, [t_y, t_xT], [t_h1 if oc == 0 else A(t_h1)])
                    P.dma("sp", hscr.rearrange("p (c t) -> p c t", c=8)[:, :, c0:c0 + 256], h1[:], [t_h1], [A(t_hscr)])

                NT1 = T // 256
                cur = s1A1(0)
                s1A2(cur)
                for ti in range(NT1):
                    s1B(cur)
                    nxt = s1A1(ti + 1) if ti + 1 < NT1 else None
                    s1C(cur)
                    if nxt is not None:
                        s1A2(nxt)
                    cur = nxt
                P.barrier()
        with contextlib.ExitStack() as st:
            W1 = sb(st, "W1", [128, 8, 4096], BF16)
            W2 = sb(st, "W2", [128, 32, 1024], BF16)
            Wpe = sb(st, "Wpe", [128, 2, 1024], BF16)
            Wpg = sb(st, "Wpg", [128, 8, 1024], BF16)
            t_w3 = P.tok()
            with contextlib.ExitStack() as stS:
                stage = Rot([(sb(stS, f"stagec{i}", [128, 2048], F32), P.tok()) for i in range(6)])
                load_w(stS, W1, w_ff1, 8, 4096, V_GPRM, stage_rot=stage, t_w=t_w3)
                load_w(stS, W2, w_ff2, 32, 1024, stage_rot=stage, t_w=t_w3)
                load_w(stS, Wpe, w_pe, 2, 1024, stage_rot=stage, t_w=t_w3)
                load_w(stS, Wpg, w_pg, 8, 1024, stage_rot=stage, t_w=t_w3)
                P.barrier()
            h_rot = Rot([(sb(st, f"h_{i}", [128, 8, 256], F32), P.tok()) for i in range(2)])
            sqA_rot = Rot([(sb(st, f"sq9_{i}", [128, 8, 256], BF16), P.tok()) for i in range(1)])
            sqC_rot = Rot([(sb(st, f"sq10_{i}", [128, 8, 256], BF16), P.tok()) for i in range(1)])
            rs_rot = Rot([(sb(st, f"rs3_{i}", [128, 256], F32), P.tok()) for i in range(3)])
            m_rot = Rot([(sb(st, f"m_{i}", [128, 8, 256], BF16), P.tok()) for i in range(1)])
            rl_rot = Rot([(sb(st, f"rl_{i}", [128, 2, 256], BF16), P.tok()) for i in range(2)])
            hh_rot = Rot([(sb(st, f"hh_{i}", [128, 8, 256], BF16), P.tok()) for i in range(2)])
            f_rot = Rot([(sb(st, f"f_{i}", [128, 8, 256], F32), P.tok()) for i in range(1)])
            pt_rot = Rot([(sb(st, f"pt_{i}", [128, 2, 256], F32), P.tok()) for i in range(1)])
            pT_rot = Rot([(sb(st, f"pT_{i}", [128, 2, 256], BF16), P.tok()) for i in range(1)])
            pg_rot = Rot([(sb(st, f"pg_{i}", [128, 256], F32), P.tok()) for i in range(2)])
            if debug:
                print('sbuf remaining sweep2', nc.sbuf_bytes_remaining)
            t_out = P.tok()
            ff_banks = Rot([(pp[0][:, j, :], P.tok()) for j in range(2)])
            c_banks = Rot([(pp[1][:, j, :], P.tok()) for j in range(2)])
            y2_banks = [(pp[2 + i][:, j, :], P.tok()) for i in range(2) for j in range(2)]
            NT2 = T // 256

            def stageA(ti):
                c0 = ti * 256
                hc, t_h = h_rot.next()
                P.dma("sp", hc[:], hscr.rearrange("p (c t) -> p c t", c=8)[:, :, c0:c0 + 256], [t_hscr], [t_h])
                sq, t_sq = sqA_rot.next()
                act(sq[:], hc[:], AF.Square, [t_h], [t_sq])
                bk, tb = c_banks.next()
                for oc in range(8):
                    mm(bk[:, 0:256], ones_bf[:], sq[:, oc, :], oc == 0, oc == 7, [t_c, t_sq], [tb])
                rs, t_rs = rs_rot.next()
                rstd_from_sum(rs[:], bk[:, 0:256], D, [tb], t_rs)
                m, t_m = m_rot.next()
                for oc in range(8):
                    tt("dve", m[:, oc, :], hc[:, oc, :], rs[:], ALU.mult, [t_h, t_rs], [t_m if oc == 0 else A(t_m)])
                return dict(hc=hc, t_h=t_h, m=m, t_m=t_m, c0=c0)

            def stageB_group(S_, fg):
                m, t_m = S_["m"], S_["t_m"]
                hh, t_hh = hh_rot.next()
                for fp in range(4):
                    bk, tb = ff_banks.next()
                    for u in range(2):
                        fc = fg * 8 + fp * 2 + u
                        for d in range(8):
                            mm(bk[:, u * 256:(u + 1) * 256], W1[:, d, fc * 128:(fc + 1) * 128], m[:, d, :], d == 0, d == 7,
                               [t_w3, t_m], [tb])
                    rl, t_rl = rl_rot.next()
                    act(rl[:], bk.rearrange("p (u t) -> p u t", u=2), AF.Relu, [tb], [t_rl])
                    tt("dve", hh[:, fp * 2:fp * 2 + 2, :], rl[:], rl[:], ALU.mult, [t_rl], [t_hh if fp == 0 else A(t_hh)])
                for oc in range(8):
                    yb, t_yb = y2_banks[oc // 2]
                    for fl in range(8):
                        fc = fg * 8 + fl
                        mm(yb[:, (oc % 2) * 256:(oc % 2) * 256 + 256], W2[:, fc, oc * 128:(oc + 1) * 128], hh[:, fl, :],
                           fc == 0 and oc % 2 == 0, fc == 31, [t_w3, t_hh], [t_yb])

            def stageB_tail(S_):
                f, t_f = f_rot.next()
                sq, t_sq = sqC_rot.next()
                for oc in range(8):
                    yb, t_yb = y2_banks[oc // 2]
                    src = yb[:, (oc % 2) * 256:(oc % 2) * 256 + 256]
                    act(f[:, oc, :], src, AF.Copy, [t_yb], [t_f if oc == 0 else A(t_f)])
                    act(sq[:, oc, :], src, AF.Square, [t_yb], [t_sq if oc == 0 else A(t_sq)])
                S_.update(f=f, t_f=t_f, sq=sq, t_sq=t_sq)

            def norm_stats(S_):
                bk, tb = c_banks.next()
                for oc in range(8):
                    mm(bk[:, 0:256], ones_bf[:], S_["sq"][:, oc, :], oc == 0, oc == 7, [t_c, S_["t_sq"]], [tb])
                rs, t_rs = rs_rot.next()
                rstd_from_sum(rs[:], bk[:, 0:256], D, [tb], t_rs)
                S_.update(rs=rs, t_rs=t_rs)

            def norm_apply(S_, gcol):
                src, t_src, hcur, t_h, rs, t_rs = S_["f"], S_["t_f"], S_["hc"], S_["t_h"], S_["rs"], S_["t_rs"]
                for oc in range(8):
                    stt("dve", src[:, oc, :], src[:, oc, :], vecs[:, gcol + oc:gcol + oc + 1], rs[:], ALU.mult, ALU.mult,
                        [t_src, t_rs, t_c], [A(t_src)])
                    tt("pool", hcur[:, oc, :], hcur[:, oc, :], src[:, oc, :], ALU.add, [t_src, t_h], [A(t_h)])

            def stageC_parts(S_):
                hc, t_h, c0 = S_["hc"], S_["t_h"], S_["c0"]
                def p1():
                    pt, t_pt = pt_rot.next()
                    P.dma("sp", pt[:], p_own[c0:c0 + 256, :].rearrange("(b p) d -> p b d", p=128), (), [t_pt])
                    norm_stats(S_)
                    pT, t_pT = pT_rot.next()
                    for b in range(2):
                        bk, tb = c_banks.next()
                        for c in range(2):
                            tr(bk[:, c * 128:(c + 1) * 128], pt[:, b, c * 128:(c + 1) * 128], identf[:], [t_pt, t_c], [tb])
                        act(pT[:, :, b * 128:(b + 1) * 128], bk[:, 0:256].rearrange("p (c t) -> p c t", c=2), AF.Copy, [tb],
                            [t_pT if b == 0 else A(t_pT)])
                    S_.update(pT=pT, t_pT=t_pT)
                    norm_apply(S_, V_GPOL)
                    act(S_["sq"][:], hc[:], AF.Copy, [t_h], [S_["t_sq"]])

                def p2():
                    hb, t_hb = S_["sq"], S_["t_sq"]
                    z, t_z = S_["f"], S_["t_f"]
                    pT, t_pT = S_["pT"], S_["t_pT"]
                    for oc in range(8):
                        bkG, tbG = c_banks.next()
                        for d in range(8):
                            mm(bkG[:, 0:256], Wpg[:, d, oc * 128:(oc + 1) * 128], hb[:, d, :], d == 0, d == 7, [t_w3, t_hb], [tbG])
                            if d == 3:
                                pass
                        for c in range(2):
                            mm(bkG[:, 256:512], Wpe[:, c, oc * 128:(oc + 1) * 128], pT[:, c, :], False, c == 1, [t_w3, t_pT], [tbG])
                        pg, t_pg = pg_rot.next()
                        act(pg[:], bkG[:, 0:256], AF.Sigmoid, [tbG], [t_pg])
                        tt("dve", z[:, oc, :], bkG[:, 256:512], pg[:], ALU.mult, [tbG, t_pg], [t_z if oc == 0 else A(t_z)])

                def p3():
                    act(S_["sq"][:], S_["f"][:], AF.Square, [S_["t_f"]], [S_["t_sq"]])
                    norm_stats(S_)
                    norm_apply(S_, V_GPLE)

                def p4():
                    fbuf, t_f = S_["f"], S_["t_f"]
                    for b in range(2):
                        ot = fbuf[:, b * 4:(b + 1) * 4, :]
                        for half in range(2):
                            bk, tb = c_banks.next()
                            for c in range(4):
                                oc = half * 4 + c
                                tr(bk[:, c * 128:(c + 1) * 128], hc[:, oc, b * 128:(b + 1) * 128], identf[:], [t_h, t_c], [tb])
                            act(ot[:, half * 2:half * 2 + 2, :], bk.rearrange("p (c t) -> p c t", c=2), AF.Copy, [tb],
                                [t_f if (b == 0 and half == 0) else A(t_f)])
                        P.dma("sp", out_own[c0 + b * 128:c0 + (b + 1) * 128, :], ot.rearrange("p c t -> p (c t)"), [t_f], [A(t_out)])
                return [p1, p2, p3, p4]

            cur = stageA(0)
            prevC = None
            for ti in range(NT2):
                for fg in range(4):
                    stageB_group(cur, fg)
                    if prevC is not None:
                        prevC[fg]()
                nxt = stageA(ti + 1) if ti + 1 < NT2 else None
                stageB_tail(cur)
                prevC = stageC_parts(cur)
                cur = nxt
            for part in prevC:
                part()
            P.barrier()
        P.emit()
    return nc


def _cols(v, n):
    return np.ascontiguousarray(np.asarray(v, np.float32).reshape(n, 128).T)


def pack_inputs(R, x, p, positions, g_pre_mix, w_in, b_gate, g_q, w_uq, g_kv, w_ukv, w_pool, pool_scale,
                w_branch_attn, w_branch_pool, w_out, g_post_mix, g_pre_mlp, w_ff1, w_ff2, g_post_mlp,
                w_ple_proj, w_ple_gate, g_ple):
    NJ = R
    S = 1024 * R
    f32 = np.float32
    x2 = np.asarray(x, f32).reshape(S, D)
    p2 = np.asarray(p, f32).reshape(S, 256)
    pos = np.asarray(positions, np.int32).reshape(S)
    w_in = np.asarray(w_in, f32)[0]
    w_uq = np.asarray(w_uq, f32)[0]
    w_ukv = np.asarray(w_ukv, f32)[0]
    vecs = np.zeros((128, V_N), f32)
    vecs[:, V_GPM:V_GPM + 8] = _cols(g_pre_mix, 8)
    vecs[:, V_GQ:V_GQ + 3] = _cols(g_q, 3)
    vecs[:, V_GKV:V_GKV + 2] = _cols(g_kv, 2)
    vecs[:, V_BG:V_BG + 16] = _cols(b_gate, 16)
    vecs[:, V_PSC:V_PSC + 4] = _cols(pool_scale, 4)
    vecs[:, V_GPOM:V_GPOM + 8] = _cols(g_post_mix, 8)
    vecs[:, V_GPRM:V_GPRM + 8] = _cols(g_pre_mlp, 8)
    vecs[:, V_GPOL:V_GPOL + 8] = _cols(g_post_mlp, 8)
    vecs[:, V_GPLE:V_GPLE + 8] = _cols(g_ple, 8)
    inv_freq = (np.float32(10000.0) ** (-np.arange(0, 32, 2, dtype=np.float32) / np.float32(32))).astype(f32)
    vecs[:, V_INVF] = np.tile(inv_freq, 8)
    wins = (2, 4, 8, 16)
    bandN = np.zeros((128, 4, 128), f32)
    band0 = np.zeros((128, 4, 128), f32)
    bandH = np.zeros((16, 4, 128), f32)
    tt_ = np.arange(128)
    for g, w in enumerate(wins):
        for t in range(128):
            for tp in range(t - w + 1, t + 1):
                if tp >= 0:
                    bandN[tp, g, t] += 1.0 / w
                    band0[tp, g, t] += 1.0 / min(t + 1, w)
                else:
                    bandH[16 + tp, g, t] += 1.0 / w
            bandN[t, g, t] -= 1.0
            band0[t, g, t] -= 1.0
    o_q, o_kv, o_kr, o_pl, o_g = 0, 384, 640, 672, 1184
    w_q = w_in[:, o_q:o_q + 384]
    w_kv = w_in[:, o_kv:o_kv + 256]
    kr = w_in[:, o_kr:o_kr + 32]
    w_krA = np.zeros((D, 96), f32)
    w_krB = np.zeros((D, 96), f32)
    w_krA[:, 64:96] = kr
    w_krB[:, 64:80] = kr[:, 16:32]
    w_krB[:, 80:96] = kr[:, 0:16]
    w_pin = w_in[:, o_pl:o_pl + 512]
    w_gate = w_in[:, o_g:o_g + 2048]
    uq = w_uq.reshape(384, 8, 96)
    w_uqA = uq.copy()
    w_uqB = uq.copy()
    w_uqB[:, :, 64:80] = uq[:, :, 80:96]
    w_uqB[:, :, 80:96] = uq[:, :, 64:80]
    ukv = w_ukv.reshape(256, 8, 128)
    w_uk = ukv[:, :, 0:64].reshape(256, 512)
    w_uv = ukv[:, :, 64:128].reshape(256, 512)
    shared = dict(
        x_all=x2, pos_all=pos.reshape(1, S), vecs=vecs,
        bandN=bandN.reshape(128, 512), bandH=bandH.reshape(16, 512),
        w_q=w_q, w_kv=w_kv, w_krA=w_krA, w_krB=w_krB, w_pin=w_pin, w_gate=w_gate,
        w_uqA=w_uqA.reshape(384, 768), w_uqB=w_uqB.reshape(384, 768), w_uk=w_uk, w_uv=w_uv,
        w_pool=np.asarray(w_pool, f32).reshape(512, 128), w_ba=np.asarray(w_branch_attn, f32)[0],
        w_bp=np.asarray(w_branch_pool, f32)[0], w_out=np.asarray(w_out, f32)[0],
        w_ff1=np.asarray(w_ff1, f32)[0], w_ff2=np.asarray(w_ff2, f32)[0],
        w_pe=np.asarray(w_ple_proj, f32)[0], w_pg=np.asarray(w_ple_gate, f32)[0],
    )
    shared = {k: np.ascontiguousarray(v) for k, v in shared.items()}
    in_maps = []
    xb = x2.reshape(NJ, 8, 128, D)
    pb = p2.reshape(NJ, 8, 128, 256)
    posb = pos.reshape(NJ, 8, 128)
    xpad = np.concatenate([np.zeros((16, D), f32), x2], 0)
    kk = np.arange(128)[:, None]
    qq = np.arange(128)[None, :]
    for c in range(NCORES):
        m = dict(shared)
        m["x_own"] = np.ascontiguousarray(xb[:, c].reshape(NJ * 128, D))
        m["p_own"] = np.ascontiguousarray(pb[:, c].reshape(NJ * 128, 256))
        m["pos_own"] = np.ascontiguousarray(posb[:, c].reshape(1, NJ * 128))
        halo = np.stack([xpad[(8 * j + c) * 128:(8 * j + c) * 128 + 16] for j in range(NJ)], 0)
        m["x_halo"] = np.ascontiguousarray(halo.reshape(NJ * 16, D))
        mk = np.full((128, 8, 128), -30000.0, f32)
        for i in range(8):
            if i < c:
                mk[:, i, :] = 0.0
            elif i == c:
                mk[:, i, :] = np.where(kk <= qq, 0.0, -30000.0)
        m["maskb"] = mk.reshape(128, 1024)
        m["band0"] = (band0 if c == 0 else bandN).reshape(128, 512).copy()
        in_maps.append(m)
    return in_maps


def unpack_output(R, results):
    NJ = R
    out = np.zeros((NJ, 8, 128, D), np.float32)
    for c in range(NCORES):
        out[:, c] = np.asarray(results[c]["out_own"], np.float32).reshape(NJ, 128, D)
    return out.reshape(1, NJ * 1024, D)


_NC_CACHE = {}


def kernel(**inputs):
    R = 16
    if R not in _NC_CACHE:
        _NC_CACHE[R] = build(R)
    nc = _NC_CACHE[R]
    in_maps = pack_inputs(R, **inputs)
    res = run_bass_kernel_spmd(nc, in_maps, core_ids=list(range(NCORES)))
    return unpack_output(R, res.results)
```
